# Optimizing a Trainium2 kernel written in Bass

```python
import math
import jax, jax.numpy as jnp
from jax import lax
import numpy as np

D_MODEL = 1024
BATCH = 16
SEQ = 2048
DEPTH = 2
DEC_BATCH = 4
DEC_SEQ = 4096
PAST_LEN = 128

GRID_W = 64
HEAD_DIM = 64
NA_HEADS = 8
NA_WIN_H = 8
NA_WIN_W = 16
GQA_HEADS = 8
GQA_KV_HEADS = 2
GQA_BLOCK = 128
ROPE_THETA = 10000.0
POOL_WINDOWS = (2, 4, 8, 16)
POOL_GROUPS = 4
POOL_CH = D_MODEL // POOL_GROUPS
N_MEM = 256
XA_HEADS = 4
XA_HEAD_DIM = D_MODEL // XA_HEADS
D_FF = 2816
CONV_W = 3
LN_EPS = 1e-5
QK_EPS = 1e-6
NEG_INF = -1e30
DN_ALPHA = (2 * DEPTH) ** 0.25
DN_BETA = (8 * DEPTH) ** -0.25
N_EVEN = (DEPTH + 1) // 2
N_ODD = DEPTH // 2
NA_WIDTH = NA_HEADS * HEAD_DIM
GQA_Q_WIDTH = GQA_HEADS * HEAD_DIM
GQA_KV_WIDTH = GQA_KV_HEADS * HEAD_DIM
AB_IN = 3 * NA_WIDTH + GQA_Q_WIDTH + 2 * GQA_KV_WIDTH
AB_OUT = NA_WIDTH + GQA_Q_WIDTH
AB_SPLITS = (NA_WIDTH, 2 * NA_WIDTH, 3 * NA_WIDTH, 3 * NA_WIDTH + GQA_Q_WIDTH,
             3 * NA_WIDTH + GQA_Q_WIDTH + GQA_KV_WIDTH)

kernel_name = "hybrid_natten_gqa_pool_encoder"


def layer_norm(x, g, b):
    xf = x.astype(jnp.float32)
    mu = jnp.mean(xf, axis=-1, keepdims=True)
    xc = xf - mu
    var = jnp.mean(xc * xc, axis=-1, keepdims=True)
    return (xc * lax.rsqrt(var + LN_EPS) * g + b).astype(x.dtype)


def rms_norm_heads(x, g):
    xf = x.astype(jnp.float32)
    return xf * lax.rsqrt(jnp.mean(xf * xf, axis=-1, keepdims=True) + QK_EPS) * g


def rope_1d(x, pos):
    half = x.shape[-1] // 2
    inv_freq = ROPE_THETA ** (-jnp.arange(half, dtype=jnp.float32) / half)
    ang = pos[:, None] * inv_freq[None, :]
    cos = jnp.cos(ang)[None, :, None, :]
    sin = jnp.sin(ang)[None, :, None, :]
    x1, x2 = x[..., :half], x[..., half:]
    return jnp.concatenate([x1 * cos - x2 * sin, x2 * cos + x1 * sin], axis=-1)


def axial_rope(x):
    S = x.shape[1]
    t = jnp.arange(S)
    row = (t // GRID_W).astype(jnp.float32)
    col = (t % GRID_W).astype(jnp.float32)
    half = HEAD_DIM // 2
    return jnp.concatenate([rope_1d(x[..., :half], row), rope_1d(x[..., half:], col)], axis=-1)


def neighbourhood_attention(q, k, v, rpb):
    B, S, H, d = q.shape
    R = S // GRID_W
    KH = min(NA_WIN_H, R)
    KW = NA_WIN_W
    r = np.arange(R)
    rs = np.clip(r - KH // 2, 0, R - KH)
    rows_idx = rs[:, None] + np.arange(KH)[None, :]
    dr_idx = rows_idx - r[:, None] + (NA_WIN_H - 1)
    c = np.arange(GRID_W)
    cs = np.clip(c - KW // 2, 0, GRID_W - KW)
    col_valid = (c[None, :] >= cs[:, None]) & (c[None, :] < cs[:, None] + KW)
    dc_idx = np.clip(c[None, :] - c[:, None] + (KW - 1), 0, 2 * KW - 2)
    bias = rpb[:, dr_idx[:, None, :, None], dc_idx[None, :, None, :]].astype(jnp.float32)
    bias = jnp.where(col_valid[None, None, :, None, :], bias, NEG_INF)
    qg = q.reshape(B, R, GRID_W, H, d)
    kg = k.reshape(B, R, GRID_W, H, d)[:, rows_idx]
    vg = v.reshape(B, R, GRID_W, H, d)[:, rows_idx]
    s = jnp.einsum('brqhd,brkwhd->bhrqkw', qg, kg,
                   preferred_element_type=jnp.float32) * (d ** -0.5) + bias[None]
    p = jax.nn.softmax(s.reshape(B, H, R, GRID_W, KH * GRID_W), axis=-1)
    p = p.reshape(B, H, R, GRID_W, KH, GRID_W).astype(v.dtype)
    o = jnp.einsum('bhrqkw,brkwhd->brqhd', p, vg)
    return o.reshape(B, S, H * d)


def gqa_axial_attention(q, k, v, q_gain, k_gain):
    B, S = q.shape[:2]
    G = GQA_HEADS // GQA_KV_HEADS
    q = axial_rope(rms_norm_heads(q, q_gain)).astype(v.dtype)
    k = axial_rope(rms_norm_heads(k, k_gain)).astype(v.dtype)
    nblk = S // GQA_BLOCK
    qb = q.reshape(B, nblk, GQA_BLOCK, GQA_KV_HEADS, G, HEAD_DIM).swapaxes(0, 1)
    scale = HEAD_DIM ** -0.5

    def attend(qblk):
        s = jnp.einsum('bqkgd,bskd->bkgqs', qblk, k, preferred_element_type=jnp.float32) * scale
        p = jax.nn.softmax(s, axis=-1).astype(v.dtype)
        return jnp.einsum('bkgqs,bskd->bqkgd', p, v)

    o = lax.map(attend, qb)
    return o.swapaxes(0, 1).reshape(B, S, GQA_HEADS * HEAD_DIM)


def mixer_ab(x, w_in, rpb, q_gain, k_gain, w_out):
    B, S, _ = x.shape
    h = x @ w_in
    qa, ka, va, qb, kb, vb = jnp.split(h, AB_SPLITS, axis=-1)
    heads = lambda t, n: t.reshape(B, S, n, HEAD_DIM)
    oa = neighbourhood_attention(heads(qa, NA_HEADS), heads(ka, NA_HEADS), heads(va, NA_HEADS), rpb)
    ob = gqa_axial_attention(heads(qb, GQA_HEADS), heads(kb, GQA_KV_HEADS),
                             heads(vb, GQA_KV_HEADS), q_gain, k_gain)
    return jnp.concatenate([oa, ob], axis=-1) @ w_out


def pool_mixer(x, w, b, scale):
    B, S, D = x.shape
    xg = x.reshape(B, S, POOL_GROUPS, POOL_CH)
    t = jnp.arange(S)
    outs = []
    for gi, win in enumerate(POOL_WINDOWS):
        xi = xg[:, :, gi, :].astype(jnp.float32)
        csum = jnp.concatenate([jnp.zeros((B, 1, POOL_CH), jnp.float32), jnp.cumsum(xi, axis=1)], axis=1)
        lo = jnp.clip(t - win // 2, 0, S - 1)
        hi = jnp.clip(t + (win - 1 - win // 2), 0, S - 1)
        cnt = (hi - lo + 1).astype(jnp.float32)[None, :, None]
        mean = (jnp.take(csum, hi + 1, axis=1) - jnp.take(csum, lo, axis=1)) / cnt
        outs.append(mean - xi)
    pooled = jnp.stack(outs, axis=2).astype(x.dtype)
    y = jnp.einsum('bsgc,gcd->bsgd', pooled, w) + b
    return y.reshape(B, S, D) * scale


def mem_cross_attention(x, mem, wq, wkv, wo):
    B, S, D = x.shape
    M = mem.shape[1]
    q = (x @ wq).reshape(B, S, XA_HEADS, XA_HEAD_DIM)
    kv = (mem @ wkv).reshape(B, M, 2, XA_HEADS, XA_HEAD_DIM)
    s = jnp.einsum('bshd,bmhd->bhsm', q, kv[:, :, 0],
                   preferred_element_type=jnp.float32) * (XA_HEAD_DIM ** -0.5)
    p = jax.nn.softmax(s, axis=-1).astype(x.dtype)
    o = jnp.einsum('bhsm,bmhd->bshd', p, kv[:, :, 1]).reshape(B, S, D)
    return o @ wo


def conv_ffn(x, w_up, conv_w, conv_b, w_down):
    h = x @ w_up
    hp = jnp.pad(h, ((0, 0), (1, 1), (0, 0)))
    h = hp[:, :-2] * conv_w[0] + hp[:, 1:-1] * conv_w[1] + hp[:, 2:] * conv_w[2] + conv_b
    val, gate = jnp.split(h, 2, axis=-1)
    return (jax.nn.gelu(gate) * val) @ w_down


def run_trunk(x, mem, ab_w_in, na_rpb, gqa_q_gain, gqa_k_gain, ab_w_out, pool_w, pool_b,
              pool_scale, ln1_g, ln1_b, xa_wq, xa_wkv, xa_wo, ln2_g, ln2_b, ffn_w_up,
              ffn_conv_w, ffn_conv_b, ffn_w_down, ln3_g, ln3_b):
    for l in range(DEPTH):
        i = l // 2
        if l % 2 == 0:
            m = mixer_ab(x, ab_w_in[i], na_rpb[i], gqa_q_gain[i], gqa_k_gain[i], ab_w_out[i])
        else:
            m = pool_mixer(x, pool_w[i], pool_b[i], pool_scale[i])
        x = layer_norm(DN_ALPHA * x + m, ln1_g[l], ln1_b[l])
        x = layer_norm(DN_ALPHA * x + mem_cross_attention(x, mem, xa_wq[l], xa_wkv[l], xa_wo[l]),
                       ln2_g[l], ln2_b[l])
        x = layer_norm(DN_ALPHA * x + conv_ffn(x, ffn_w_up[l], ffn_conv_w[l], ffn_conv_b[l], ffn_w_down[l]),
                       ln3_g[l], ln3_b[l])
    return x


def setup_inputs(seed: int = 0) -> dict:
    key = jax.random.key(seed)
    ks = jax.random.split(key, 26)
    f32 = jnp.float32
    nrm = lambda k, shape, s: jax.random.normal(k, shape, f32) * s
    D = D_MODEL
    return {
        "x_prompt": nrm(ks[0], (BATCH, SEQ, D), 1.0),
        "x_sample": nrm(ks[1], (DEC_BATCH, DEC_SEQ, D), 1.0),
        "mem_prompt": nrm(ks[2], (BATCH, N_MEM, D), 1.0),
        "mem_sample": nrm(ks[3], (DEC_BATCH, N_MEM, D), 1.0),
        "ab_w_in": nrm(ks[4], (N_EVEN, D, AB_IN), D ** -0.5),
        "na_rpb": nrm(ks[5], (N_EVEN, NA_HEADS, 2 * NA_WIN_H - 1, 2 * NA_WIN_W - 1), 0.1),
        "gqa_q_gain": 1.0 + nrm(ks[6], (N_EVEN, HEAD_DIM), 0.02),
        "gqa_k_gain": 1.0 + nrm(ks[7], (N_EVEN, HEAD_DIM), 0.02),
        "ab_w_out": nrm(ks[8], (N_EVEN, AB_OUT, D), AB_OUT ** -0.5 * DN_BETA),
        "pool_w": nrm(ks[9], (N_ODD, POOL_GROUPS, POOL_CH, POOL_CH), POOL_CH ** -0.5 * DN_BETA),
        "pool_b": nrm(ks[10], (N_ODD, POOL_GROUPS, POOL_CH), 0.01),
        "pool_scale": 1.0 + nrm(ks[11], (N_ODD, D), 0.02),
        "ln1_g": 1.0 + nrm(ks[12], (DEPTH, D), 0.02),
        "ln1_b": nrm(ks[13], (DEPTH, D), 0.02),
        "xa_wq": nrm(ks[14], (DEPTH, D, D), D ** -0.5),
        "xa_wkv": nrm(ks[15], (DEPTH, D, 2 * D), D ** -0.5),
        "xa_wo": nrm(ks[16], (DEPTH, D, D), D ** -0.5 * DN_BETA),
        "ln2_g": 1.0 + nrm(ks[17], (DEPTH, D), 0.02),
        "ln2_b": nrm(ks[18], (DEPTH, D), 0.02),
        "ffn_w_up": nrm(ks[19], (DEPTH, D, 2 * D_FF), D ** -0.5),
        "ffn_conv_w": nrm(ks[20], (DEPTH, CONV_W, 2 * D_FF), CONV_W ** -0.5),
        "ffn_conv_b": nrm(ks[21], (DEPTH, 2 * D_FF), 0.02),
        "ffn_w_down": nrm(ks[22], (DEPTH, D_FF, D), D_FF ** -0.5 * DN_BETA),
        "ln3_g": 1.0 + nrm(ks[23], (DEPTH, D), 0.02),
        "ln3_b": nrm(ks[24], (DEPTH, D), 0.02),
    }


def reference(x_prompt, x_sample, mem_prompt, mem_sample, ab_w_in, na_rpb, gqa_q_gain, gqa_k_gain,
              ab_w_out, pool_w, pool_b, pool_scale, ln1_g, ln1_b, xa_wq, xa_wkv, xa_wo, ln2_g, ln2_b,
              ffn_w_up, ffn_conv_w, ffn_conv_b, ffn_w_down, ln3_g, ln3_b):
    y_prompt = run_trunk(x_prompt, mem_prompt, ab_w_in, na_rpb, gqa_q_gain, gqa_k_gain, ab_w_out,
                         pool_w, pool_b, pool_scale, ln1_g, ln1_b, xa_wq, xa_wkv, xa_wo, ln2_g, ln2_b,
                         ffn_w_up, ffn_conv_w, ffn_conv_b, ffn_w_down, ln3_g, ln3_b)
    y_sample = run_trunk(x_sample, mem_sample, ab_w_in, na_rpb, gqa_q_gain, gqa_k_gain, ab_w_out,
                         pool_w, pool_b, pool_scale, ln1_g, ln1_b, xa_wq, xa_wkv, xa_wo, ln2_g, ln2_b,
                         ffn_w_up, ffn_conv_w, ffn_conv_b, ffn_w_down, ln3_g, ln3_b)
    return (y_prompt, y_sample)
```

```python
import numpy as np
import concourse.bass as bass
import concourse.mybir as mybir
from concourse.bass_utils import run_bass_kernel_spmd

F32 = mybir.dt.float32
BF16 = mybir.dt.bfloat16
ALU = mybir.AluOpType
AF = mybir.ActivationFunctionType

NCORES = 8
D = 1024
DFF = 2816
NPAIR = 22
ALPHA = float((2 * 2) ** 0.25)
LN_EPS = 1e-5
QK_EPS = 1e-6
NEG = -1e30
GRID_W = 64

SMAX = 2176
TBMAX = 440
XSZ = 8 * (SMAX + 16) * 4
OSZ = 8 * SMAX * 2
OFF_X = 0
OFF_O = XSZ
OFF_W = XSZ + OSZ
ARENA_BYTES = 204800
WSZ = ARENA_BYTES - OFF_W


class Tok:
    __slots__ = ("sem", "val", "eng")

    def __init__(self, sem, val, eng):
        self.sem, self.val, self.eng = sem, val, eng


class Buf:
    __slots__ = ("name", "w", "r")

    def __init__(self, name):
        self.name, self.w, self.r = name, None, {}


class Builder:
    EPOCH = 30000

    def __init__(self):
        self.nc = bass.Bass("TRN2", target_bir_lowering=False)
        nc = self.nc
        self.engs = {"pe": nc.tensor, "act": nc.scalar, "dve": nc.vector, "pool": nc.gpsimd, "sp": nc.sync}
        self.esem = {}
        self.ecnt = {}
        self.last = {}
        self.known = {e: {} for e in self.engs}
        self.nsem = 0
        for e in ("pe", "act", "dve", "pool"):
            self._new_epoch(e)
        self.dslots = {q: [[self._sem(), 0, None] for _ in range(n)] for q, n in (("sp", 10), ("pool", 14))}
        self.dnext = {"sp": 0, "pool": 0}
        self.ninst = 0

    def _sem(self):
        self.nsem += 1
        return self.nc.alloc_semaphore(f"s{self.nsem}")

    def _new_epoch(self, e):
        self.esem[e] = self._sem()
        self.ecnt[e] = 0

    def _wait(self, e, t):
        if t is None:
            return
        k = self.known[e]
        key = id(t.sem)
        if k.get(key, 0) >= t.val:
            return
        self.engs[e].wait_ge(t.sem, t.val)
        k[key] = t.val

    def _deps(self, e, reads, writes):
        for b in reads:
            t = b.w
            if t is not None and not (t.eng == e and e == "pe"):
                self._wait(e, t)
        for b in writes:
            t = b.w
            if t is not None and not (t.eng == e and e == "pe"):
                self._wait(e, t)
            for re_, rt in b.r.items():
                if not (re_ == e and e == "pe"):
                    self._wait(e, rt)

    def _commit(self, tok, reads, writes):
        for b in reads:
            b.r[tok.eng] = tok
        for b in writes:
            b.w = tok
            b.r = {}

    def group(self, e, fns, reads=(), writes=()):
        self._deps(e, reads, writes)
        eng = self.engs[e]
        ins = None
        for f in fns:
            ins = f(eng)
        self.ninst += len(fns)
        if self.ecnt[e] >= self.EPOCH:
            self._new_epoch(e)
        self.ecnt[e] += 1
        ins.then_inc(self.esem[e], 1)
        tok = Tok(self.esem[e], self.ecnt[e], e)
        self.last[e] = tok
        self._commit(tok, reads, writes)
        return tok

    def op(self, e, f, reads=(), writes=()):
        return self.group(e, [f], reads, writes)

    def dma(self, q, out, in_, reads=(), writes=()):
        slots = self.dslots[q]
        sl = slots[self.dnext[q] % len(slots)]
        self.dnext[q] += 1
        self._wait(q, sl[2])
        self._deps(q, reads, writes)
        ins = self.engs[q].dma_start(out=out, in_=in_)
        sl[1] += 16
        ins.then_inc(sl[0], 16)
        tok = Tok(sl[0], sl[1], "dma_" + q)
        sl[2] = tok
        self.ninst += 1
        self._commit(tok, reads, writes)
        return tok

    def barrier(self):
        toks = list(self.last.values())
        for q in self.dslots:
            toks += [sl[2] for sl in self.dslots[q] if sl[2] is not None]
        for e in self.engs:
            for t in toks:
                if t.eng != e:
                    self._wait(e, t)


def _blocks(S, nb=5):
    base, rem = divmod(S, nb)
    out, t0 = [], 0
    for i in range(nb):
        T = base + (1 if i < rem else 0)
        out.append((t0, T))
        t0 += T
    return out


def _na_plan(kind):
    plan = []
    if kind == "prompt":
        R = 32
        for r in range(R):
            rs = min(max(r - 4, 0), R - 8)
            lo, hi, je = rs, rs + 7, r
            plan.append(_tiles(lo, hi, je))
    else:
        nQr = 34
        for jq in range(nQr):
            je = jq + 4
            lo, hi = je - 4, je + 3
            if jq < 4:
                hi = 11
            if jq >= 31:
                lo = 30
            plan.append(_tiles(lo, hi, je))
    return plan


def _tiles(lo, hi, je):
    out = []
    for i in range(lo // 2, hi // 2 + 1):
        lov = lo <= 2 * i <= hi
        hiv = lo <= 2 * i + 1 <= hi
        jx = 2 * i - je + 8
        assert 0 <= jx <= 15
        out.append((i, lov, hiv, jx))
    return out


class Seg:
    pass


def build_program(dbg=None):
    B = Builder()
    nc = B.nc

    small = dbg is not None and dbg.startswith(("na", "consts"))

    def din(name, shape):
        if small and name.startswith(("w_gq", "w_ao", "w_x", "w_up", "w_dn", "w_pool")):
            shape = [2, 2]
        return nc.dram_tensor(name, list(shape), F32, kind="ExternalInput").ap()

    def dout(name, shape):
        return nc.dram_tensor(name, list(shape), F32, kind="ExternalOutput").ap()

    segs = []
    for i, (nm, S, nE, nK, qoff, kind) in enumerate(
            [("p0", 2048, 16, 16, 0, "prompt"), ("p1", 2048, 16, 16, 0, "prompt"), ("s", 2176, 21, 34, 256, "sample")]):
        sg = Seg()
        sg.name, sg.S, sg.nE, sg.nK, sg.qoff, sg.kind = nm, S, nE, nK, qoff, kind
        sg.xT = din("xT_" + nm, [D, nK * 128]).rearrange("(k p) t -> p k t", p=128)
        sg.memT = din("memT_" + nm, [128, 8 * 256])
        sg.yT = dout("yT_" + nm, [D, S]).rearrange("(k p) t -> p k t", p=128)
        sg.blocks = _blocks(S)
        sg.plan = _na_plan(kind)
        segs.append(sg)
    rope_p = din("rope_p", [128, 2, 2048])
    ropeq_s = din("ropeq_s", [128, 2, 2176])
    ropek_s = din("ropek_s", [128, 2, 34 * 128])
    na_g = din("na_g", [128, 8 * 1024])
    na_m_p = din("na_m_p", [128, 1024]); na_n_p = din("na_n_p", [128, 1024])
    na_m_s = din("na_m_s", [128, 1024]); na_n_s = din("na_n_s", [128, 1024])
    exE_s = din("exE_s", [128, 21]); exK_s = din("exK_s", [128, 34])
    for sg in segs:
        if sg.kind == "prompt":
            sg.ropeq, sg.ropek, sg.na_m, sg.na_n, sg.exE, sg.exK = rope_p, rope_p, na_m_p, na_n_p, None, None
        else:
            sg.ropeq, sg.ropek, sg.na_m, sg.na_n, sg.exE, sg.exK = ropeq_s, ropek_s, na_m_s, na_n_s, exE_s, exK_s
    w_na = din("w_na", [128, 8 * 1536])
    w_gq = din("w_gq", [128, 8 * 1664])
    w_ao = din("w_ao", [128, 8 * 1024])
    w_xq = [din(f"w_xq{l}", [128, 8 * 1024]) for l in range(2)]
    w_xo = [din(f"w_xo{l}", [128, 8 * 1024]) for l in range(2)]
    w_xkv = [din(f"w_xkv{l}", [8, 128, 8 * 256]) for l in range(2)]
    w_up = [din(f"w_up{l}", [NPAIR, 128, 8 * 256]) for l in range(2)]
    w_dn = [din(f"w_dn{l}", [8, 128, NPAIR * 128]) for l in range(2)]
    w_pool = din("w_pool", [128, 4 * 2 * 256])
    consts = din("consts", [128, 3 * 128])
    pvec = din("pvec", [128, 512])
    rc_edge = din("rc_edge", [128, 8 * 16])
    dbg_out = dout("dbg", [D, SMAX]).rearrange("(k p) t -> p k t", p=128) if dbg else None
    wup_bf = [nc.dram_tensor(f"wupbf{l}", [NPAIR, 128, 8 * 256], BF16).ap() for l in range(2)]
    wdn_bf = [nc.dram_tensor(f"wdnbf{l}", [8, 128, NPAIR * 128], BF16).ap() for l in range(2)]
    b_upc = [[Buf(f"upc{l}_{j}") for j in range(NPAIR)] for l in range(2)]
    b_dnc = [[Buf(f"dnc{l}_{j}") for j in range(8)] for l in range(2)]
    conv_jobs = [("up", l, j) for l in range(2) for j in range(NPAIR)] + [("dn", l, j) for l in range(2) for j in range(8)]
    conv_jobs.sort(key=lambda t: t[1])

    def emit_conv(n):
        for _ in range(n):
            if not conv_jobs or small:
                return
            kind, l, j = conv_jobs.pop(0)
            if kind == "up":
                B.dma("pool", wup_bf[l][j], w_up[l][j], writes=[b_upc[l][j]])
            else:
                B.dma("pool", wdn_bf[l][j], w_dn[l][j], writes=[b_dnc[l][j]])

    ARENA = nc.alloc_sbuf_tensor("arena", [128, ARENA_BYTES // 4], F32)
    PS = nc.alloc_psum_tensor("ps", [128, 8, 512], F32)
    CONB = nc.alloc_sbuf_tensor("conb", [128, 3 * 128], BF16)
    ONESM = nc.alloc_sbuf_tensor("onesm", [128, 128], BF16)
    ONES1 = nc.alloc_sbuf_tensor("ones1", [128, 128], BF16)
    NEGH = nc.alloc_sbuf_tensor("negh", [128, 512], F32)
    NEGB = nc.alloc_sbuf_tensor("negb", [128, 64], BF16)
    PV = nc.alloc_sbuf_tensor("pvec_sb", [128, 512], F32)
    RCE = nc.alloc_sbuf_tensor("rce", [128, 8, 16], F32)
    DER = nc.alloc_sbuf_tensor("der", [128, 8], F32)
    IDENT = CONB[:, 0:128]
    BD = CONB[:, 128:256]

    def view(off, shape, dt):
        esz = 2 if dt == BF16 else 4
        n = int(np.prod(shape[1:]))
        assert off % 4 == 0 and (n * esz) % 4 == 0
        ap = ARENA[:, off // 4: (off + n * esz) // 4]
        if dt == BF16:
            ap = ap.bitcast(BF16)
        if len(shape) == 3:
            ap = ap.rearrange("p (a b) -> p a b", a=shape[1])
        elif len(shape) == 4:
            ap = ap.rearrange("p (a b c) -> p a b c", a=shape[1], b=shape[2])
        return ap

    class Alloc:
        def __init__(self, base, size):
            self.base, self.size, self.off = base, size, 0

        def get(self, shape, dt):
            esz = 2 if dt == BF16 else 4
            n = int(np.prod(shape[1:])) * esz
            n = (n + 3) // 4 * 4
            v = view(self.base + self.off, shape, dt)
            self.off += n
            assert self.off <= self.size, (self.off, self.size)
            return v

    bank_b = [Buf(f"bank{i}") for i in range(8)]
    b_const = Buf("const")

    PV_LN = 0
    PV_CW = 96
    PV_CB = 360
    PV_PB = 448
    PV_PS = 456
    PV_GN = 464

    def ln_gb(l, n, c):
        o = PV_LN + ((l * 3 + n) * 2) * 8
        return PV[:, o + c: o + c + 1], PV[:, o + 8 + c: o + 8 + c + 1]

    B.dma("pool", CONB[:, :], consts[:, :], writes=[b_const])
    B.dma("sp", PV[:, :], pvec[:, :], writes=[b_const])
    B.dma("sp", RCE[:, :, :], rc_edge.rearrange("p (c e) -> p c e", c=8), writes=[b_const])
    B.op("dve", lambda e: e.memset(ONESM[:, :], 1.0 / 1024.0), writes=[b_const])
    B.op("dve", lambda e: e.memset(ONES1[:, :], 1.0), writes=[b_const])
    B.op("dve", lambda e: e.memset(NEGH[:, :], -0.5), writes=[b_const])
    B.op("dve", lambda e: e.memset(NEGB[:, :], NEG), writes=[b_const])
    B.op("dve", lambda e: e.tensor_tensor(out=DER[:, :], in0=PV[:, PV_PB:PV_PB + 8], in1=PV[:, PV_PS:PV_PS + 8],
                                           op=ALU.mult), reads=[b_const], writes=[b_const])
    B.barrier()

    rot = {}

    def rbank(key, banks):
        i = rot.get(key, 0)
        rot[key] = i + 1
        return banks[i % len(banks)]

    def split_dma(q, out2d, in2d, ncols, name, piece=4096):
        bufs = []
        for c0 in range(0, ncols, piece):
            c1 = min(ncols, c0 + piece)
            b = Buf(name)
            B.dma(q, out2d[:, c0:c1], in2d[:, c0:c1], writes=[b])
            bufs.append(b)
        return bufs

    def ln_steps(zw, T, l, n, Xv, xb, t0, zi, banks=(6, 7)):
        Z, ZB, ZSQ, MEAN, VAR, RSTD, b_z, b_zb, b_st = zw
        Z, b_z = Z[zi], b_z[zi]
        bs, bq = banks

        def pe_sum(src, bk):
            B.group("pe", [(lambda e, c=c: e.matmul(PS[:, bk, :T], lhsT=ONESM[:, :], rhs=src[:, c, :T], start=(c == 0), stop=(c == 7)))
                           for c in range(8)], reads=[b_zb, b_const], writes=[bank_b[bk]])

        def s1():
            B.op("act", lambda e: e.activation(out=ZB[:, :, :T], in_=Z[:, :, :T], func=AF.Identity), reads=[b_z], writes=[b_zb])
            B.op("act", lambda e: e.activation(out=ZSQ[:, :, :T], in_=Z[:, :, :T], func=AF.Square), reads=[b_z], writes=[b_zb])
            pe_sum(ZB, bs)
            if bq != bs:
                pe_sum(ZSQ, bq)

        def s2():
            B.op("dve", lambda e: e.tensor_copy(out=MEAN[:, :T], in_=PS[:, bs, :T]), reads=[bank_b[bs]], writes=[b_st])
            B.op("dve", lambda e: e.tensor_tensor(out=VAR[:, :T], in0=MEAN[:, :T], in1=MEAN[:, :T], op=ALU.mult), reads=[b_st], writes=[b_st])
            if bq == bs:
                pe_sum(ZSQ, bq)
            B.op("dve", lambda e: e.tensor_tensor(out=VAR[:, :T], in0=PS[:, bq, :T], in1=VAR[:, :T], op=ALU.subtract),
                 reads=[bank_b[bq], b_st], writes=[b_st])
            B.op("dve", lambda e: e.tensor_scalar(out=VAR[:, :T], in0=VAR[:, :T], scalar1=1.0, scalar2=LN_EPS, op0=ALU.mult, op1=ALU.add),
                 reads=[b_st], writes=[b_st])
            B.op("act", lambda e: e.activation(out=VAR[:, :T], in_=VAR[:, :T], func=AF.Ln), reads=[b_st], writes=[b_st])
            B.op("act", lambda e: e.activation(out=RSTD[:, :T], in_=VAR[:, :T], func=AF.Exp, scale=-0.5), reads=[b_st], writes=[b_st])

        def s3():
            B.op("dve", lambda e: e.tensor_tensor(out=Z[:, :, :T], in0=Z[:, :, :T], in1=MEAN[:, :T].unsqueeze(1).to_broadcast([128, 8, T]),
                                                   op=ALU.subtract), reads=[b_z, b_st], writes=[b_z])
            B.op("dve", lambda e: e.tensor_tensor(out=Z[:, :, :T], in0=Z[:, :, :T], in1=RSTD[:, :T].unsqueeze(1).to_broadcast([128, 8, T]),
                                                   op=ALU.mult), reads=[b_z, b_st], writes=[b_z])

        def s4():
            fns = []
            for c in range(8):
                g, bb = ln_gb(l, n, c)
                fns.append(lambda e, c=c, g=g, bb=bb: e.activation(out=Xv[:, c, 8 + t0: 8 + t0 + T], in_=Z[:, c, :T],
                                                                   func=AF.Identity, bias=bb, scale=g))
            B.group("act", fns, reads=[b_z, b_const], writes=[xb])
        return [s1, s2, s3, s4]

    def ln_block(zw, T, l, n, Xv, xb, t0, zi, extra_reads=()):
        for st in ln_steps(zw, T, l, n, Xv, xb, t0, zi):
            st()

    def zwork(al):
        Z = [al.get([128, 8, TBMAX], F32) for _ in range(2)]
        ZB = al.get([128, 8, TBMAX], BF16)
        ZSQ = al.get([128, 8, TBMAX], BF16)
        MEAN = al.get([128, TBMAX], F32)
        VAR = al.get([128, TBMAX], F32)
        RSTD = al.get([128, TBMAX], F32)
        return (Z, ZB, ZSQ, MEAN, VAR, RSTD, [Buf("z0"), Buf("z1")], Buf("zb"), Buf("zst"))

    out_toks = []
    if dbg == "consts":
        B.dma("sp", dbg_out[:, 0, 0:512], PV[:, :], reads=[b_const])
        segs_run = []
    else:
        segs_run = segs
    for sg in segs_run:
        S, nE, nK, qoff, blocks = sg.S, sg.nE, sg.nK, sg.qoff, sg.blocks
        nQr = S // 64
        Xv = view(OFF_X, [128, 8, S + 16], F32)
        OT = view(OFF_O, [128, 8, S], BF16)
        b_x = [Buf(f"x{i}") for i in range(len(blocks))]
        b_xpad = Buf("xpad")
        b_ot = Buf("ot")

        ax = Alloc(OFF_X, XSZ)
        aw = Alloc(OFF_W, WSZ)
        VAN = ax.get([128, nE, 8, 128], BF16)
        QA = ax.get([128, 4, S], BF16)
        PTn = [ax.get([128, 2, 384], BF16) for _ in range(2)]
        SF = [ax.get([128, 2, 384], F32) for _ in range(2)]
        Wna = aw.get([128, 8, 1536], BF16)
        BT = aw.get([128, 8, 16, 64], BF16)
        XS = [aw.get([128, 8, 512], BF16) for _ in range(2)]
        KA = aw.get([128, 4, nE * 128], BF16)
        RECn = aw.get([128, 512], F32)
        MK01 = aw.get([128, 1024], F32)
        NEGT = aw.get([128, 1024], F32)
        EXT = aw.get([128, 64], F32)
        PTn.append(aw.get([128, 2, 384], BF16))
        SF.append(aw.get([128, 2, 384], F32))
        GT = view(OFF_O, [128, 8, 1024], F32)
        b_ka, b_va, b_qa, b_w, b_bt, b_rec = (Buf(n) for n in "ka va qa w bt rec".split())
        b_xs = [Buf("xs0"), Buf("xs1")]
        b_pt = [Buf("pt0"), Buf("pt1"), Buf("pt2")]
        b_sf = [Buf("sf0"), Buf("sf1"), Buf("sf2")]
        b_gt = b_ot

        bw_l = split_dma("pool", Wna.rearrange("p k n -> p (k n)"), w_na, 8 * 1536, "wna")
        for q4 in range(4):
            B.dma("sp", GT[:, 2 * q4:2 * q4 + 2, :].rearrange("p h n -> p (h n)"), na_g[:, q4 * 2048:(q4 + 1) * 2048], writes=[b_gt])
        B.dma("sp", MK01[:, :], sg.na_m[:, :], writes=[b_gt])
        B.dma("sp", NEGT[:, :], sg.na_n[:, :], writes=[b_gt])
        B.op("dve", lambda e: e.tensor_tensor(out=GT[:, :, :], in0=GT[:, :, :], in1=MK01[:, :].unsqueeze(1).to_broadcast([128, 8, 1024]),
                                               op=ALU.mult), reads=[b_gt], writes=[b_gt])
        B.op("dve", lambda e: e.tensor_tensor(out=BT.rearrange("p h j q -> p h (j q)"), in0=GT[:, :, :],
                                               in1=NEGT[:, :].unsqueeze(1).to_broadcast([128, 8, 1024]), op=ALU.add),
             reads=[b_gt], writes=[b_bt])
        if sg.exE is None:
            B.op("dve", lambda e: e.memset(VAN[:, :, :, 64:128], 1.0), writes=[b_va])
        else:
            B.dma("sp", EXT[:, :nE], sg.exE[:, :], writes=[b_gt])
            for hq in range(8):
                B.op("dve", lambda e, hq=hq: e.tensor_copy(out=VAN[:, :, hq, 64:128], in_=EXT[:, :nE].unsqueeze(2).to_broadcast([128, nE, 64])),
                     reads=[b_gt], writes=[b_va])

        nxs = [0]

        def load_xs(c0, w):
            i = nxs[0] % 2
            nxs[0] += 1
            B.dma("pool", XS[i][:, :, :w], sg.xT[:, :, c0:c0 + w], writes=[b_xs[i]])
            return XS[i], b_xs[i]

        for c0 in range(0, nE * 128, 512):
            w = min(512, nE * 128 - c0)
            xs, bxs = load_xs(c0, w)
            for c in range(4):
                bk = rbank("a", [0, 1, 2, 3])
                B.group("pe", [(lambda e, k=k, c=c, bk=bk: e.matmul(PS[:, bk, :w], lhsT=Wna[:, k, 512 + c * 128: 512 + (c + 1) * 128],
                                                                    rhs=xs[:, k, :w], start=(k == 0), stop=(k == 7))) for k in range(8)],
                        reads=bw_l + [bxs], writes=[bank_b[bk]])
                B.op("act", lambda e, c=c, bk=bk: e.activation(out=KA[:, c, c0:c0 + w], in_=PS[:, bk, :w], func=AF.Identity),
                     reads=[bank_b[bk]], writes=[b_ka])
            for tt in range(w // 128):
                bk = rbank("a", [0, 1, 2, 3])
                B.group("pe", [(lambda e, k=k, tt=tt, bk=bk: e.matmul(PS[:, bk, :], lhsT=xs[:, k, tt * 128:(tt + 1) * 128],
                                                                      rhs=Wna[:, k, 1024:1536], start=(k == 0), stop=(k == 7))) for k in range(8)],
                        reads=bw_l + [bxs], writes=[bank_b[bk]])
                B.op("dve", lambda e, tt=tt, bk=bk: e.tensor_copy(out=VAN[:, c0 // 128 + tt, :, 0:64],
                                                                  in_=PS[:, bk, :].rearrange("p (h d) -> p h d", h=8)),
                     reads=[bank_b[bk]], writes=[b_va])
        for (t0, T) in blocks:
            xs, bxs = load_xs(qoff + t0, T)
            for c in range(4):
                bk = rbank("a", [0, 1, 2, 3])
                B.group("pe", [(lambda e, k=k, c=c, bk=bk: e.matmul(PS[:, bk, :T], lhsT=Wna[:, k, c * 128:(c + 1) * 128],
                                                                    rhs=xs[:, k, :T], start=(k == 0), stop=(k == 7))) for k in range(8)],
                        reads=bw_l + [bxs], writes=[bank_b[bk]])
                B.op("act", lambda e, c=c, bk=bk: e.activation(out=QA[:, c, t0:t0 + T], in_=PS[:, bk, :T], func=AF.Identity, scale=0.125),
                     reads=[bank_b[bk]], writes=[b_qa])

        if dbg == "na_kv" and sg is segs[DBG_SEG]:
            for c in range(4):
                B.dma("pool", dbg_out[:, c, :S], QA[:, c, :], reads=[b_qa])
                B.dma("pool", dbg_out[:, 4 + c, :S], KA[:, c, :S], reads=[b_ka])
            break
        def na_scores(jq, hp):
            tiles = sg.plan[jq]
            q0 = jq * 64
            sb = rbank("nas", [0, 2, 6])
            nt = len(tiles)
            fns = []
            for hh in range(2):
                h, c, bp = 2 * hp + hh, hp, hh * 64
                I64 = IDENT[bp:bp + 64, bp:bp + 64]
                for ti, (i, lov, hiv, jx) in enumerate(tiles):
                    if lov and hiv:
                        p0, p1, k0, kn, fill = 0, 128, i * 128, 128, None
                    elif lov:
                        p0, p1, k0, kn, fill = 0, 64, i * 128, 64, (64, 128)
                    else:
                        p0, p1, k0, kn, fill = 64, 128, i * 128 + 64, 64, (0, 64)
                    o = PS[p0:p1, sb + hh, ti * 64:(ti + 1) * 64]
                    fns.append(lambda e, o=o, c=c, bp=bp, k0=k0, kn=kn: e.matmul(
                        o, lhsT=KA[bp:bp + 64, c, k0:k0 + kn], rhs=QA[bp:bp + 64, c, q0:q0 + 64], start=True, stop=True))
                    if fill is not None:
                        of = PS[fill[0]:fill[1], sb + hh, ti * 64:(ti + 1) * 64]
                        fns.append(lambda e, of=of, bp=bp, I64=I64: e.matmul(
                            of, lhsT=NEGB[bp:bp + 64, 0:64], rhs=I64, start=True, stop=True))
            B.group("pe", fns, reads=[b_ka, b_qa, b_const], writes=[bank_b[sb], bank_b[sb + 1]])
            pi = rbank("napt", [0, 1, 2])
            jx0 = tiles[0][3]
            for hh in range(2):
                h = 2 * hp + hh
                B.op("dve", lambda e, hh=hh, h=h: e.tensor_tensor(
                    out=SF[pi][:, hh, :nt * 64].rearrange("p (t q) -> p t q", q=64), in0=PS[:, sb + hh, :nt * 64].rearrange("p (t q) -> p t q", q=64),
                    in1=BT[:, h, jx0:jx0 + 2 * nt - 1:2, :], op=ALU.add),
                    reads=[bank_b[sb + hh], b_bt], writes=[b_sf[pi]])
            B.op("act", lambda e: e.activation(out=PTn[pi][:, :, :nt * 64], in_=SF[pi][:, :, :nt * 64], func=AF.Exp),
                 reads=[b_sf[pi]], writes=[b_pt[pi]])
            return (jq, hp, pi)

        def na_pv(item, ob):
            jq, hp, pi = item
            tiles = sg.plan[jq]
            nt = len(tiles)
            fns = []
            for hh in range(2):
                h = 2 * hp + hh
                for ti, (i, lov, hiv, jx) in enumerate(tiles):
                    rhs = PTn[pi][:, hh, ti * 64:(ti + 1) * 64]
                    fns.append(lambda e, rhs=rhs, i=i, h=h, ti=ti: e.matmul(
                        PS[:, ob, h * 64:(h + 1) * 64], lhsT=VAN[:, i, h, :], rhs=rhs, start=(ti == 0), stop=(ti == nt - 1)))
            B.group("pe", fns, reads=[b_va, b_pt[pi]], writes=[bank_b[ob]])

        def na_norm(jq, ob):
            q0 = jq * 64
            B.op("act", lambda e: e.activation(out=RECn[0:64, :], in_=PS[64:128, ob, :], func=AF.Ln), reads=[bank_b[ob]], writes=[b_rec])
            B.op("act", lambda e: e.activation(out=RECn[0:64, :], in_=RECn[0:64, :], func=AF.Exp, scale=-1.0), reads=[b_rec], writes=[b_rec])
            pv = PS[0:64, ob, :].rearrange("p (c hh q) -> p c hh q", hh=2, q=64)
            rv = RECn[0:64, :].rearrange("p (c hh q) -> p c hh q", hh=2, q=64)
            B.op("dve", lambda e: e.tensor_tensor(out=OT[0:64, 0:4, q0:q0 + 64], in0=pv[:, :, 0, :], in1=rv[:, :, 0, :], op=ALU.mult),
                 reads=[bank_b[ob], b_rec], writes=[b_ot])
            B.op("dve", lambda e: e.tensor_tensor(out=OT[64:128, 0:4, q0:q0 + 64], in0=pv[:, :, 1, :], in1=rv[:, :, 1, :], op=ALU.mult),
                 reads=[bank_b[ob], b_rec], writes=[b_ot])

        pendq = []

        def na_retire():
            it = pendq.pop(0)
            na_pv(it, 4 + (it[0] % 2))
            if it[1] == 3:
                na_norm(it[0], 4 + (it[0] % 2))

        for jq in range(nQr):
            for hp in range(4):
                if hp % 2 == 0 and conv_jobs and conv_jobs[0][1] == 0:
                    emit_conv(1)
                pendq.append(na_scores(jq, hp))
                if len(pendq) > 2:
                    na_retire()
        while pendq:
            na_retire()
        B.barrier()
        if dbg == "na" and sg is segs[DBG_SEG]:
            for c in range(4):
                B.dma("pool", dbg_out[:, c, :S], OT[:, c, :], reads=[b_ot])
            break

        ax = Alloc(OFF_X, XSZ)
        aw = Alloc(OFF_W, WSZ)
        QB = ax.get([128, 4, S], BF16)
        KB = ax.get([128, 2, nK * 128], BF16)
        VAUG = ax.get([128, nK, 2, 128], BF16)
        PTg = [ax.get([128, 2, 512], BF16) for _ in range(3)]
        SQ_ = [ax.get([128, 512], BF16)]
        RR_ = [ax.get([128, 512], F32)]
        T1_ = [ax.get([128, 512], F32)]
        T2_ = [ax.get([128, 512], F32)]
        Wg = aw.get([128, 8, 1664], BF16)
        XS = [aw.get([128, 8, 512], BF16) for _ in range(2)]
        CS = [aw.get([128, 2, 512], F32) for _ in range(2)]
        CG = aw.get([128, 512], F32)
        SG = aw.get([128, 512], F32)
        RECg = aw.get([128, 512], F32)
        EXT = aw.get([128, 64], F32)
        SQ_.append(aw.get([128, 512], BF16)); RR_.append(aw.get([128, 512], F32))
        T1_.append(aw.get([128, 512], F32)); T2_.append(aw.get([128, 512], F32))
        b_qb, b_kb, b_vaug, b_w, b_cg, b_tmp, b_rec = (Buf(n) for n in "qb kb vaug wg cg tmp recg".split())
        b_tmp2 = [Buf("tmpA"), Buf("tmpB")]
        qkn = [0]
        b_xs = [Buf("xs0"), Buf("xs1")]
        b_cs = [Buf("cs0"), Buf("cs1")]
        b_pt = [Buf("ptg0"), Buf("ptg1"), Buf("ptg2")]
        bw_l = split_dma("pool", Wg.rearrange("p k n -> p (k n)"), w_gq, 8 * 1664, "wg")
        if sg.exK is None:
            B.op("dve", lambda e: e.memset(VAUG[:, :, :, 64:128], 1.0), writes=[b_vaug])
        else:
            B.dma("sp", EXT[:, :nK], sg.exK[:, :], writes=[b_tmp])
            B.op("dve", lambda e: e.tensor_copy(out=VAUG[:, :, 0, 64:128], in_=EXT[:, :nK].unsqueeze(2).to_broadcast([128, nK, 64])),
                 reads=[b_tmp], writes=[b_vaug])
            B.op("dve", lambda e: e.tensor_copy(out=VAUG[:, :, 1, 64:128], in_=EXT[:, :nK].unsqueeze(2).to_broadcast([128, nK, 64])),
                 reads=[b_tmp], writes=[b_vaug])
        QOFF, QROFF, KOFF, KROFF, VOFF = 0, 512, 1024, 1280, 1536
        ncs = [0]

        def load_cs(tab, c0, w, gcol):
            i = ncs[0] % 2
            ncs[0] += 1
            B.dma("sp", CS[i][:, :, :w], tab[:, :, c0:c0 + w], writes=[b_cs[i]])
            B.op("dve", lambda e: e.tensor_scalar(out=CG[:, :w], in0=CS[i][:, 0, :w], scalar1=PV[:, PV_GN + gcol:PV_GN + gcol + 1],
                                                   scalar2=None, op0=ALU.mult), reads=[b_cs[i], b_const], writes=[b_cg])
            B.op("dve", lambda e: e.tensor_scalar(out=SG[:, :w], in0=CS[i][:, 1, :w], scalar1=PV[:, PV_GN + gcol + 1:PV_GN + gcol + 2],
                                                   scalar2=None, op0=ALU.mult), reads=[b_cs[i], b_const], writes=[b_cg])

        def qk_proj(xs, bxs, w, col, rcol, out_ap, b_out, scale):
            par = qkn[0] % 2
            qkn[0] += 1
            SQ, RR, T1, T2, b_tmp = SQ_[par], RR_[par], T1_[par], T2_[par], b_tmp2[par]
            ba = rbank("ga", [0, 1])
            bb = rbank("gb", [2, 3])
            bc = rbank("gc", [4, 5])
            B.group("pe", [(lambda e, k=k: e.matmul(PS[:, ba, :w], lhsT=Wg[:, k, col:col + 128], rhs=xs[:, k, :w],
                                                     start=(k == 0), stop=(k == 7))) for k in range(8)],
                    reads=bw_l + [bxs], writes=[bank_b[ba]])
            B.group("pe", [(lambda e, k=k: e.matmul(PS[:, bb, :w], lhsT=Wg[:, k, rcol:rcol + 128], rhs=xs[:, k, :w],
                                                     start=(k == 0), stop=(k == 7))) for k in range(8)],
                    reads=bw_l + [bxs], writes=[bank_b[bb]])
            B.op("act", lambda e: e.activation(out=SQ[:, :w], in_=PS[:, ba, :w], func=AF.Square), reads=[bank_b[ba]], writes=[b_tmp])
            B.op("pe", lambda e: e.matmul(PS[:, bc, :w], lhsT=BD, rhs=SQ[:, :w], start=True, stop=True),
                 reads=[b_tmp, b_const], writes=[bank_b[bc]])
            B.op("dve", lambda e: e.tensor_scalar(out=RR[:, :w], in0=PS[:, bc, :w], scalar1=1.0 / 64.0, scalar2=QK_EPS,
                                                   op0=ALU.mult, op1=ALU.add), reads=[bank_b[bc]], writes=[b_tmp])
            B.op("act", lambda e: e.activation(out=RR[:, :w], in_=RR[:, :w], func=AF.Ln), reads=[b_tmp], writes=[b_tmp])
            B.op("act", lambda e: e.activation(out=RR[:, :w], in_=RR[:, :w], func=AF.Exp, scale=-0.5), reads=[b_tmp], writes=[b_tmp])
            B.op("dve", lambda e: e.tensor_tensor(out=T1[:, :w], in0=PS[:, ba, :w], in1=CG[:, :w], op=ALU.mult),
                 reads=[bank_b[ba], b_cg], writes=[b_tmp])
            B.op("dve", lambda e: e.tensor_tensor(out=T2[:, :w], in0=PS[:, bb, :w], in1=SG[:, :w], op=ALU.mult),
                 reads=[bank_b[bb], b_cg], writes=[b_tmp])
            B.op("dve", lambda e: e.tensor_tensor(out=T1[:, :w], in0=T1[:, :w], in1=T2[:, :w], op=ALU.add), reads=[b_tmp], writes=[b_tmp])
            B.op("dve", lambda e: e.scalar_tensor_tensor(out=out_ap, in0=T1[:, :w], scalar=scale, in1=RR[:, :w],
                                                          op0=ALU.mult, op1=ALU.mult), reads=[b_tmp], writes=[b_out])

        nxs[0] = 0
        for c0 in range(0, nK * 128, 512):
            w = min(512, nK * 128 - c0)
            xs, bxs = load_xs(c0, w)
            load_cs(sg.ropek, c0, w, 2)
            for g in range(2):
                qk_proj(xs, bxs, w, KOFF + g * 128, KROFF + g * 128, KB[:, g, c0:c0 + w], b_kb, 1.0)
            for tt in range(w // 128):
                bk = rbank("gv", [6, 7])
                B.group("pe", [(lambda e, k=k, tt=tt, bk=bk: e.matmul(PS[:, bk, 0:128], lhsT=xs[:, k, tt * 128:(tt + 1) * 128],
                                                                      rhs=Wg[:, k, VOFF:VOFF + 128], start=(k == 0), stop=(k == 7))) for k in range(8)],
                        reads=bw_l + [bxs], writes=[bank_b[bk]])
                B.op("act", lambda e, tt=tt, bk=bk: e.activation(out=VAUG[:, c0 // 128 + tt, :, 0:64],
                                                                 in_=PS[:, bk, 0:128].rearrange("p (g d) -> p g d", g=2), func=AF.Identity),
                     reads=[bank_b[bk]], writes=[b_vaug])
        for (t0, T) in blocks:
            xs, bxs = load_xs(qoff + t0, T)
            load_cs(sg.ropeq, t0, T, 0)
            for c in range(4):
                qk_proj(xs, bxs, T, QOFF + c * 128, QROFF + c * 128, QB[:, c, t0:t0 + T], b_qb, 0.125)

        def gq_scores(t0, T, c, kg):
            g = c // 2
            sb = rbank("gqs", [0, 2])
            B.group("pe", [(lambda e, hh=hh: e.matmul(PS[:, sb + hh, :T], lhsT=KB[hh * 64:(hh + 1) * 64, g, kg * 128:(kg + 1) * 128],
                                                       rhs=QB[hh * 64:(hh + 1) * 64, c, t0:t0 + T], start=True, stop=True)) for hh in range(2)],
                    reads=[b_kb, b_qb], writes=[bank_b[sb], bank_b[sb + 1]])
            pi = rbank("gqpt", [0, 1, 2])
            B.op("act", lambda e: e.activation(out=PTg[pi][:, :, :T], in_=PS[:, sb:sb + 2, :T], func=AF.Exp),
                 reads=[bank_b[sb], bank_b[sb + 1]], writes=[b_pt[pi]])
            return (t0, T, c, kg, pi)

        def gq_pv(item):
            t0, T, c, kg, pi = item
            g = c // 2
            ob = 4 + 2 * (c % 2)
            B.group("pe", [(lambda e, hh=hh: e.matmul(PS[:, ob + hh, :T], lhsT=VAUG[:, kg, g, :], rhs=PTg[pi][:, hh, :T],
                                                       start=(kg == 0), stop=(kg == nK - 1))) for hh in range(2)],
                    reads=[b_vaug, b_pt[pi]], writes=[bank_b[ob], bank_b[ob + 1]])
            if kg == nK - 1:
                for hh in range(2):
                    bp = hh * 64
                    B.op("dve", lambda e, hh=hh: e.reciprocal(out=RECg[0:64, :T], in_=PS[64:128, ob + hh, :T]), reads=[bank_b[ob + hh]], writes=[b_rec])
                    B.op("dve", lambda e, hh=hh, bp=bp: e.tensor_tensor(out=OT[bp:bp + 64, 4 + c, t0:t0 + T], in0=PS[0:64, ob + hh, :T],
                                                                        in1=RECg[0:64, :T], op=ALU.mult),
                         reads=[bank_b[ob + hh], b_rec], writes=[b_ot])

        pend = []
        for (t0, T) in blocks:
            for c in range(4):
                for kg in range(nK):
                    if kg % 4 == 0:
                        emit_conv(1)
                    pend.append(gq_scores(t0, T, c, kg))
                    if len(pend) > 1:
                        gq_pv(pend.pop(0))
        while pend:
            gq_pv(pend.pop(0))
        B.barrier()

        if dbg == "attn" and sg is segs[DBG_SEG]:
            for c in range(8):
                B.dma("pool", dbg_out[:, c, :S], OT[:, c, :], reads=[b_ot])
            break

        HI = 20480

        def xattn_prep(l):
            ah = Alloc(OFF_W + WSZ - HI, HI)
            MT = ah.get([128, 8, 256], BF16)
            MK = ah.get([128, 8, 256], BF16)
            MV = ah.get([128, 2, 1024], BF16)
            SL = [ah.get([128, 8, 256], BF16) for _ in range(2)]
            b_mt, b_mk, b_mv = Buf("mt"), Buf("mk"), Buf("mv")
            b_sl = [Buf("sl0"), Buf("sl1")]
            B.dma("pool", MT.rearrange("p k n -> p (k n)"), sg.memT[:, :], writes=[b_mt])
            for s in range(8):
                B.dma("pool", SL[s % 2].rearrange("p k n -> p (k n)"), w_xkv[l][s], writes=[b_sl[s % 2]])
                sl = SL[s % 2]
                if s < 4:
                    for cc in range(2):
                        bk = rbank("x", [0, 1])
                        B.group("pe", [(lambda e, k=k, cc=cc, bk=bk: e.matmul(PS[:, bk, :256], lhsT=sl[:, k, cc * 128:(cc + 1) * 128],
                                                                              rhs=MT[:, k, :], start=(k == 0), stop=(k == 7))) for k in range(8)],
                                reads=[b_sl[s % 2], b_mt], writes=[bank_b[bk]])
                        B.op("act", lambda e, cc=cc, bk=bk: e.activation(out=MK[:, 2 * s + cc, :], in_=PS[:, bk, :256], func=AF.Identity),
                             reads=[bank_b[bk]], writes=[b_mk])
                else:
                    for mt in range(2):
                        bk = rbank("x", [0, 1])
                        B.group("pe", [(lambda e, k=k, mt=mt, bk=bk: e.matmul(PS[:, bk, :256], lhsT=MT[:, k, mt * 128:(mt + 1) * 128],
                                                                              rhs=sl[:, k, :], start=(k == 0), stop=(k == 7))) for k in range(8)],
                                reads=[b_sl[s % 2], b_mt], writes=[bank_b[bk]])
                        B.op("act", lambda e, mt=mt, bk=bk: e.activation(out=MV[:, mt, (s - 4) * 256:(s - 3) * 256], in_=PS[:, bk, :256],
                                                                         func=AF.Identity), reads=[bank_b[bk]], writes=[b_mv])
            return MK, MV, b_mk, b_mv

        def stage_xattn(l, prep):
            MK, MV, b_mk, b_mv = prep
            ao = Alloc(OFF_O, OSZ)
            aw = Alloc(OFF_W, WSZ - HI)
            Wq = ao.get([128, 8, 1024], BF16)
            Wo = ao.get([128, 8, 1024], BF16)
            zw = zwork(aw)
            XBb = aw.get([128, 8, TBMAX], BF16)
            QX = aw.get([128, 8, TBMAX], BF16)
            OX = aw.get([128, 8, TBMAX], BF16)
            PTx = [aw.get([128, 2, TBMAX], BF16) for _ in range(2)]
            REC = aw.get([128, TBMAX], F32)
            b_xbb, b_qx, b_ox, b_rec = (Buf(n) for n in "xbb qx ox rec".split())
            b_pt = [Buf("ptx0"), Buf("ptx1")]
            bwq_l = split_dma("pool", Wq.rearrange("p k n -> p (k n)"), w_xq[l], 8192, "wq")
            bwo_l = split_dma("pool", Wo.rearrange("p k n -> p (k n)"), w_xo[l], 8192, "wo")

            def qphase(bi):
                t0, T = blocks[bi]
                B.op("act", lambda e: e.activation(out=XBb[:, :, :T], in_=Xv[:, :, 8 + t0:8 + t0 + T], func=AF.Identity),
                     reads=[b_x[bi]], writes=[b_xbb])
                for ch in range(8):
                    bk = rbank("x", [0, 1])
                    B.group("pe", [(lambda e, k=k, ch=ch, bk=bk: e.matmul(PS[:, bk, :T], lhsT=Wq[:, k, ch * 128:(ch + 1) * 128],
                                                                          rhs=XBb[:, k, :T], start=(k == 0), stop=(k == 7))) for k in range(8)],
                            reads=bwq_l + [b_xbb], writes=[bank_b[bk]])
                    B.op("act", lambda e, ch=ch, bk=bk: e.activation(out=QX[:, ch, :T], in_=PS[:, bk, :T], func=AF.Identity, scale=1.0 / 16.0),
                         reads=[bank_b[bk]], writes=[b_qx])

            qphase(0)
            pend_ln = []
            for bi, (t0, T) in enumerate(blocks):
                Z, b_z = zw[0][bi % 2], zw[6][bi % 2]
                def xa_scores(h):
                    sb = 0 if h % 2 == 0 else 2
                    fns = []
                    for mt in range(2):
                        for dc in range(2):
                            fns.append(lambda e, mt=mt, dc=dc: e.matmul(PS[:, sb + mt, :T], lhsT=MK[:, 2 * h + dc, mt * 128:(mt + 1) * 128],
                                                                        rhs=QX[:, 2 * h + dc, :T], start=(dc == 0), stop=(dc == 1)))
                    B.group("pe", fns, reads=[b_mk, b_qx], writes=[bank_b[sb], bank_b[sb + 1]])
                    pi = h % 2
                    B.op("act", lambda e: e.activation(out=PTx[pi][:, :, :T], in_=PS[:, sb:sb + 2, :T], func=AF.Exp),
                         reads=[bank_b[sb], bank_b[sb + 1]], writes=[b_pt[pi]])

                def xa_pv(h):
                    pi = h % 2
                    fns = []
                    for dc in range(2):
                        for mt in range(2):
                            fns.append(lambda e, mt=mt, dc=dc: e.matmul(PS[:, 4 + dc, :T], lhsT=MV[:, mt, h * 256 + dc * 128:h * 256 + (dc + 1) * 128],
                                                                        rhs=PTx[pi][:, mt, :T], start=(mt == 0), stop=(mt == 1)))
                    B.group("pe", fns, reads=[b_mv, b_pt[pi]], writes=[bank_b[4], bank_b[5]])
                    B.group("pe", [(lambda e, mt=mt: e.matmul(PS[:, 6, :T], lhsT=ONES1[:, :], rhs=PTx[pi][:, mt, :T],
                                                               start=(mt == 0), stop=(mt == 1))) for mt in range(2)],
                            reads=[b_const, b_pt[pi]], writes=[bank_b[6]])
                    B.op("act", lambda e: e.activation(out=REC[:, :T], in_=PS[:, 6, :T], func=AF.Ln), reads=[bank_b[6]], writes=[b_rec])
                    B.op("act", lambda e: e.activation(out=REC[:, :T], in_=REC[:, :T], func=AF.Exp, scale=-1.0), reads=[b_rec], writes=[b_rec])
                    B.op("dve", lambda e: e.tensor_tensor(out=OX[:, 2 * h:2 * h + 2, :T], in0=PS[:, 4:6, :T],
                                                           in1=REC[:, :T].unsqueeze(1).to_broadcast([128, 2, T]), op=ALU.mult),
                         reads=[bank_b[4], bank_b[5], b_rec], writes=[b_ox])

                pend_h = None
                for h in range(4):
                    xa_scores(h)
                    if pend_h is not None:
                        xa_pv(pend_h)
                    pend_h = h
                    if pend_ln:
                        pend_ln.pop(0)()
                xa_pv(pend_h)
                while pend_ln:
                    pend_ln.pop(0)()
                for oc in range(8):
                    bk = rbank("x", [0, 1])
                    B.group("pe", [(lambda e, k=k, oc=oc, bk=bk: e.matmul(PS[:, bk, :T], lhsT=Wo[:, k, oc * 128:(oc + 1) * 128],
                                                                          rhs=OX[:, k, :T], start=(k == 0), stop=(k == 7))) for k in range(8)],
                            reads=bwo_l + [b_ox], writes=[bank_b[bk]])
                    B.op("dve", lambda e, oc=oc, bk=bk: e.scalar_tensor_tensor(out=Z[:, oc, :T], in0=Xv[:, oc, 8 + t0:8 + t0 + T], scalar=ALPHA,
                                                                               in1=PS[:, bk, :T], op0=ALU.mult, op1=ALU.add),
                         reads=[b_x[bi], bank_b[bk]], writes=[b_z])
                if bi + 1 < len(blocks):
                    qphase(bi + 1)
                pend_ln = ln_steps(zw, T, l, 1, Xv, b_x[bi], t0, bi % 2, banks=(7, 7))
            while pend_ln:
                pend_ln.pop(0)()
            B.barrier()

        def stage_ffn(l, final):
            emit_conv(1000)
            ao = Alloc(OFF_O, OSZ)
            aw = Alloc(OFF_W, WSZ)
            GTt = ao.get([128, NPAIR, TBMAX], BF16)
            zw = zwork(aw)
            XBf = [aw.get([128, 8, TBMAX + 2], BF16) for _ in range(2)]
            UP = [aw.get([128, 8, 256], BF16) for _ in range(3)] + [ao.get([128, 8, 256], BF16)]
            DN = [aw.get([128, NPAIR, 128], BF16) for _ in range(2)] + [ao.get([128, NPAIR, 128], BF16) for _ in range(2)]
            CV = [aw.get([128, TBMAX], F32) for _ in range(3)]
            CGt = [aw.get([128, TBMAX], F32) for _ in range(2)]
            GL = [aw.get([128, TBMAX], F32) for _ in range(2)]
            HAL = aw.get([128, 8, 2 * 8], BF16)
            b_hal = Buf("hal")
            b_gt, b_xbf = Buf("gtt"), [Buf("xbf0"), Buf("xbf1")]
            b_up = [Buf(f"up{i}") for i in range(4)]
            b_dn = [Buf(f"dn{i}") for i in range(4)]
            b_cv = [Buf("cv0"), Buf("cv1"), Buf("cv2")]
            b_cg2 = [Buf("cg0"), Buf("cg1")]
            b_gl = [Buf("gl0"), Buf("gl1")]
            nblk = len(blocks)
            up_seq = [(bi, j) for bi in range(nblk) for j in range(NPAIR)]
            dn_seq = [(bi, oc) for bi in range(nblk) for oc in range(8)]
            up_iss, dn_iss = [0], [0]

            def up_prefetch(upto):
                while up_iss[0] <= min(upto, len(up_seq) - 1):
                    i = up_iss[0]
                    B.dma("sp", UP[i % 4].rearrange("p k n -> p (k n)"), wup_bf[l][up_seq[i][1]], reads=[b_upc[l][up_seq[i][1]]], writes=[b_up[i % 4]])
                    up_iss[0] += 1

            def dn_prefetch(upto):
                while dn_iss[0] <= min(upto, len(dn_seq) - 1):
                    i = dn_iss[0]
                    B.dma("sp", DN[i % 4].rearrange("p k n -> p (k n)"), wdn_bf[l][dn_seq[i][1]], reads=[b_dnc[l][dn_seq[i][1]]], writes=[b_dn[i % 4]])
                    dn_iss[0] += 1

            def cw(ch, tap):
                o = PV_CW + (l * 44 + ch) * 3 + tap
                return PV[:, o:o + 1]

            def cb(ch):
                o = PV_CB + l * 44 + ch
                return PV[:, o:o + 1]

            def conv(bk, ch, dst, bdst, T):
                B.op("act", lambda e: e.activation(out=dst[:, :T], in_=PS[:, bk, 1:T + 1], func=AF.Identity, bias=cb(ch), scale=cw(ch, 1)),
                     reads=[bank_b[bk], b_const], writes=[bdst])
                B.op("dve", lambda e: e.scalar_tensor_tensor(out=dst[:, :T], in0=PS[:, bk, 0:T], scalar=cw(ch, 0), in1=dst[:, :T],
                                                              op0=ALU.mult, op1=ALU.add), reads=[bank_b[bk], bdst, b_const], writes=[bdst])
                B.op("dve", lambda e: e.scalar_tensor_tensor(out=dst[:, :T], in0=PS[:, bk, 2:T + 2], scalar=cw(ch, 2), in1=dst[:, :T],
                                                              op0=ALU.mult, op1=ALU.add), reads=[bank_b[bk], bdst, b_const], writes=[bdst])

            for bi, (t0, T) in enumerate(blocks):
                rd = [b_xpad] + ([b_x[bi - 1]] if bi > 0 else []) + ([b_x[bi + 1]] if bi + 1 < nblk else [])
                B.op("act", lambda e: e.activation(out=HAL[:, :, 2 * bi:2 * bi + 1], in_=Xv[:, :, 8 + t0 - 1:8 + t0], func=AF.Identity),
                     reads=rd, writes=[b_hal])
                B.op("act", lambda e: e.activation(out=HAL[:, :, 2 * bi + 1:2 * bi + 2], in_=Xv[:, :, 8 + t0 + T:8 + t0 + T + 1], func=AF.Identity),
                     reads=rd, writes=[b_hal])
            def cast_xbf(bi):
                t0, T = blocks[bi]
                xbf, bx = XBf[bi % 2], b_xbf[bi % 2]
                B.op("act", lambda e: e.activation(out=xbf[:, :, 1:T + 1], in_=Xv[:, :, 8 + t0:8 + t0 + T], func=AF.Identity),
                     reads=[b_x[bi]], writes=[bx])
                B.op("act", lambda e: e.activation(out=xbf[:, :, 0:1], in_=HAL[:, :, 2 * bi:2 * bi + 1], func=AF.Identity),
                     reads=[b_hal], writes=[bx])
                B.op("act", lambda e: e.activation(out=xbf[:, :, T + 1:T + 2], in_=HAL[:, :, 2 * bi + 1:2 * bi + 2], func=AF.Identity),
                     reads=[b_hal], writes=[bx])

            cast_xbf(0)
            deferred = []
            for bi, (t0, T) in enumerate(blocks):
                Z, b_z = zw[0][bi % 2], zw[6][bi % 2]
                xbf, bx = XBf[bi % 2], b_xbf[bi % 2]

                def bpair(j):
                    return ((0, 1), (2, 3), (6, 7))[j % 3]

                def emit_pe(j):
                    ui = bi * NPAIR + j
                    up_prefetch(ui + 3)
                    up, bup = UP[ui % 4], b_up[ui % 4]
                    bv, bg = bpair(j)
                    B.group("pe", [(lambda e, k=k: e.matmul(PS[:, bv, :T + 2], lhsT=up[:, k, 0:128], rhs=xbf[:, k, :T + 2],
                                                             start=(k == 0), stop=(k == 7))) for k in range(8)],
                            reads=[bup, bx], writes=[bank_b[bv]])
                    B.group("pe", [(lambda e, k=k: e.matmul(PS[:, bg, :T + 2], lhsT=up[:, k, 128:256], rhs=xbf[:, k, :T + 2],
                                                             start=(k == 0), stop=(k == 7))) for k in range(8)],
                            reads=[bup, bx], writes=[bank_b[bg]])

                def emit_center(j):
                    s2, s3 = j % 2, j % 3
                    bv, bg = bpair(j)
                    chv, chg = j, NPAIR + j
                    B.op("act", lambda e: e.activation(out=CV[s3][:, :T], in_=PS[:, bv, 1:T + 1], func=AF.Identity, bias=cb(chv), scale=cw(chv, 1)),
                         reads=[bank_b[bv], b_const], writes=[b_cv[s3]])
                    B.op("act", lambda e: e.activation(out=CGt[s2][:, :T], in_=PS[:, bg, 1:T + 1], func=AF.Identity, bias=cb(chg), scale=cw(chg, 1)),
                         reads=[bank_b[bg], b_const], writes=[b_cg2[s2]])

                def emit_taps_gelu(j):
                    s2, s3 = j % 2, j % 3
                    bv, bg = bpair(j)
                    chv, chg = j, NPAIR + j
                    for tap, lo in ((0, 0), (2, 2)):
                        B.op("dve", lambda e, tap=tap, lo=lo: e.scalar_tensor_tensor(out=CV[s3][:, :T], in0=PS[:, bv, lo:lo + T], scalar=cw(chv, tap),
                                                                                     in1=CV[s3][:, :T], op0=ALU.mult, op1=ALU.add),
                             reads=[bank_b[bv], b_cv[s3], b_const], writes=[b_cv[s3]])
                        B.op("dve", lambda e, tap=tap, lo=lo: e.scalar_tensor_tensor(out=CGt[s2][:, :T], in0=PS[:, bg, lo:lo + T], scalar=cw(chg, tap),
                                                                                     in1=CGt[s2][:, :T], op0=ALU.mult, op1=ALU.add),
                             reads=[bank_b[bg], b_cg2[s2], b_const], writes=[b_cg2[s2]])
                    B.op("act", lambda e: e.activation(out=GL[s2][:, :T], in_=CGt[s2][:, :T], func=AF.Gelu_apprx_tanh),
                         reads=[b_cg2[s2]], writes=[b_gl[s2]])

                def emit_mult(jm):
                    B.op("dve", lambda e: e.tensor_tensor(out=GTt[:, jm, :T], in0=GL[jm % 2][:, :T], in1=CV[jm % 3][:, :T], op=ALU.mult),
                         reads=[b_gl[jm % 2], b_cv[jm % 3]], writes=[b_gt])

                dn_prefetch(bi * 8 + 1)
                emit_pe(0)
                emit_pe(1)
                emit_center(0)
                for j in range(NPAIR):
                    if j + 2 < NPAIR:
                        emit_pe(j + 2)
                    if j + 1 < NPAIR:
                        emit_center(j + 1)
                    emit_taps_gelu(j)
                    if j >= 1:
                        emit_mult(j - 1)
                    if j in (2, 4, 6, 8, 10) and deferred:
                        deferred.pop(0)()
                emit_mult(NPAIR - 1)
                if bi + 1 < nblk:
                    cast_xbf(bi + 1)
                for oc in range(8):
                    di = bi * 8 + oc
                    dn_prefetch(di + 3)
                    dn, bdn = DN[di % 4], b_dn[di % 4]
                    bk = 4 + (oc % 2)
                    B.group("pe", [(lambda e, k=k: e.matmul(PS[:, bk, :T], lhsT=dn[:, k, :], rhs=GTt[:, k, :T],
                                                             start=(k == 0), stop=(k == NPAIR - 1))) for k in range(NPAIR)],
                            reads=[bdn, b_gt], writes=[bank_b[bk]])
                    B.op("dve", lambda e, oc=oc, bk=bk: e.scalar_tensor_tensor(out=Z[:, oc, :T], in0=Xv[:, oc, 8 + t0:8 + t0 + T], scalar=ALPHA,
                                                                               in1=PS[:, bk, :T], op0=ALU.mult, op1=ALU.add),
                         reads=[b_x[bi], bank_b[bk]], writes=[b_z])

                def tail(bi=bi, t0=t0, T=T):
                    if final:
                        out_toks.append(B.dma("sp", sg.yT[:, :, t0:t0 + T], Xv[:, :, 8 + t0:8 + t0 + T], reads=[b_x[bi]]))
                deferred.extend(ln_steps(zw, T, l, 2, Xv, b_x[bi], t0, bi % 2, banks=(4, 5)))
                deferred.append(tail)
            while deferred:
                deferred.pop(0)()
            B.barrier()

        aw = Alloc(OFF_W, WSZ - HI)
        Wao = aw.get([128, 8, 1024], BF16)
        zw = zwork(aw)
        b_wao = Buf("wao")
        bwao_l = split_dma("pool", Wao.rearrange("p k n -> p (k n)"), w_ao, 8192, "wao")
        prep0 = xattn_prep(0)
        B.op("dve", lambda e: e.memset(Xv[:, :, 0:8], 0.0), writes=[b_xpad])
        B.op("dve", lambda e: e.memset(Xv[:, :, 8 + S:16 + S], 0.0), writes=[b_xpad])
        for bi, (t0, T) in enumerate(blocks):
            B.dma("sp", Xv[:, :, 8 + t0:8 + t0 + T], sg.xT[:, :, qoff + t0:qoff + t0 + T], writes=[b_x[bi]])
        def outproj_b(bi):
            t0, T = blocks[bi]
            Z, b_z = zw[0][bi % 2], zw[6][bi % 2]
            for oc in range(8):
                bk = rbank("b", [0, 1, 2, 3])
                B.group("pe", [(lambda e, k=k, oc=oc, bk=bk: e.matmul(PS[:, bk, :T], lhsT=Wao[:, k, oc * 128:(oc + 1) * 128],
                                                                      rhs=OT[:, k, t0:t0 + T], start=(k == 0), stop=(k == 7))) for k in range(8)],
                        reads=bwao_l + [b_ot], writes=[bank_b[bk]])
                B.op("dve", lambda e, oc=oc, bk=bk: e.scalar_tensor_tensor(out=Z[:, oc, :T], in0=Xv[:, oc, 8 + t0:8 + t0 + T], scalar=ALPHA,
                                                                           in1=PS[:, bk, :T], op0=ALU.mult, op1=ALU.add),
                     reads=[b_x[bi], bank_b[bk]], writes=[b_z])

        outproj_b(0)
        for bi, (t0, T) in enumerate(blocks):
            if bi + 1 < len(blocks):
                outproj_b(bi + 1)
            ln_block(zw, T, 0, 0, Xv, b_x[bi], t0, bi % 2)
        B.barrier()

        def dump_x():
            for c in range(8):
                B.dma("sp", dbg_out[:, c, :S], Xv[:, c, 8:8 + S], reads=b_x)

        if dbg == "ln1" and sg is segs[DBG_SEG]:
            dump_x(); break
        stage_xattn(0, prep0)
        if dbg == "ln2" and sg is segs[DBG_SEG]:
            dump_x(); break
        stage_ffn(0, False)
        if dbg == "l0" and sg is segs[DBG_SEG]:
            dump_x(); break

        ao = Alloc(OFF_O, OSZ)
        aw = Alloc(OFF_W, WSZ)
        PB = ao.get([128, 8, S], BF16)
        WT = [aw.get([128, 2, TBMAX + 16], F32) for _ in range(4)]
        b_pb = Buf("pb")
        b_wt = Buf("wt")
        WINS = (2, 4, 8, 16)
        nblk = len(blocks)
        for bi, (t0, T) in enumerate(blocks):
            rd = [b_x[bi], b_xpad] + ([b_x[bi - 1]] if bi > 0 else []) + ([b_x[bi + 1]] if bi + 1 < nblk else [])
            for gi in range(4):
                cs = slice(2 * gi, 2 * gi + 2)
                X0 = 8 + t0
                ext = {0: (0,), 1: (1, 0), 2: (3, 2, 0), 3: (7, 6, 4, 0)}[gi]
                e1 = ext[0]
                n1 = T + 2 * e1
                B.op("dve", lambda e: e.tensor_tensor(out=WT[0][:, :, :n1], in0=Xv[:, cs, X0 - e1 - 1:X0 - e1 - 1 + n1],
                                                       in1=Xv[:, cs, X0 - e1:X0 - e1 + n1], op=ALU.add), reads=rd + [b_wt], writes=[b_wt])
                cur, cure = WT[0], e1
                for lev in range(1, gi + 1):
                    sh = 2 ** (lev - 1)
                    en = ext[lev]
                    nn = T + 2 * en
                    o0 = cure - en
                    nxt = WT[lev]
                    B.op("dve", lambda e, cur=cur, nxt=nxt, o0=o0, sh=sh, nn=nn: e.tensor_tensor(
                        out=nxt[:, :, :nn], in0=cur[:, :, o0 - sh:o0 - sh + nn], in1=cur[:, :, o0 + sh:o0 + sh + nn], op=ALU.add),
                        reads=[b_wt], writes=[b_wt])
                    cur, cure = nxt, en
                B.op("dve", lambda e, cur=cur: e.scalar_tensor_tensor(out=PB[:, cs, t0:t0 + T], in0=cur[:, :, :T], scalar=1.0 / WINS[gi],
                                                                       in1=Xv[:, cs, X0:X0 + T], op0=ALU.mult, op1=ALU.subtract),
                     reads=rd + [b_wt], writes=[b_pb])
                if bi == 0:
                    B.op("dve", lambda e, cur=cur: e.tensor_tensor(out=cur[:, :, 0:8], in0=cur[:, :, 0:8], in1=RCE[:, cs, 0:8], op=ALU.mult),
                         reads=[b_wt, b_const], writes=[b_wt])
                    B.op("dve", lambda e, cur=cur: e.tensor_tensor(out=PB[:, cs, 0:8], in0=cur[:, :, 0:8], in1=Xv[:, cs, X0:X0 + 8], op=ALU.subtract),
                         reads=rd + [b_wt], writes=[b_pb])
                if bi == nblk - 1:
                    B.op("dve", lambda e, cur=cur: e.tensor_tensor(out=cur[:, :, T - 8:T], in0=cur[:, :, T - 8:T], in1=RCE[:, cs, 8:16], op=ALU.mult),
                         reads=[b_wt, b_const], writes=[b_wt])
                    B.op("dve", lambda e, cur=cur: e.tensor_tensor(out=PB[:, cs, S - 8:S], in0=cur[:, :, T - 8:T], in1=Xv[:, cs, X0 + T - 8:X0 + T],
                                                                    op=ALU.subtract), reads=rd + [b_wt], writes=[b_pb])
        B.barrier()
        aw = Alloc(OFF_W, WSZ)
        zw = zwork(aw)
        PW = aw.get([128, 4, 2, 256], BF16)
        MT4 = [aw.get([128, TBMAX], F32) for _ in range(4)]
        b_pw, b_mt4 = Buf("pw"), [Buf(f"mt{i}") for i in range(4)]
        B.dma("pool", PW.rearrange("p g k n -> p (g k n)"), w_pool[:, :], writes=[b_pw])
        prep1 = xattn_prep(1)
        def pool_mm(bi):
            t0, T = blocks[bi]
            Z, b_z = zw[0][bi % 2], zw[6][bi % 2]
            for oc in range(8):
                gi, ol = oc // 2, oc % 2
                bk = rbank("b", [0, 1, 2, 3])
                MT_, b_mt_ = MT4[oc % 4], b_mt4[oc % 4]
                B.group("pe", [(lambda e, kc=kc: e.matmul(PS[:, bk, :T], lhsT=PW[:, gi, kc, ol * 128:(ol + 1) * 128],
                                                           rhs=PB[:, 2 * gi + kc, t0:t0 + T], start=(kc == 0), stop=(kc == 1))) for kc in range(2)],
                        reads=[b_pw, b_pb], writes=[bank_b[bk]])
                B.op("act", lambda e: e.activation(out=MT_[:, :T], in_=PS[:, bk, :T], func=AF.Identity, bias=DER[:, oc:oc + 1],
                                                   scale=PV[:, PV_PS + oc:PV_PS + oc + 1]), reads=[bank_b[bk], b_const], writes=[b_mt_])
                B.op("dve", lambda e: e.scalar_tensor_tensor(out=Z[:, oc, :T], in0=Xv[:, oc, 8 + t0:8 + t0 + T], scalar=ALPHA,
                                                              in1=MT_[:, :T], op0=ALU.mult, op1=ALU.add),
                     reads=[b_x[bi], b_mt_], writes=[b_z])

        pool_mm(0)
        for bi, (t0, T) in enumerate(blocks):
            if bi + 1 < len(blocks):
                pool_mm(bi + 1)
            ln_block(zw, T, 1, 0, Xv, b_x[bi], t0, bi % 2)
        B.barrier()
        if dbg == "l1ln1" and sg is segs[DBG_SEG]:
            dump_x(); break
        stage_xattn(1, prep1)
        stage_ffn(1, True)

    B.barrier()
    return B


DBG_SEG = 0


def _kp(w):
    K = w.shape[0] // 128
    return np.ascontiguousarray(w.reshape(K, 128, w.shape[1]).transpose(1, 0, 2).reshape(128, -1))


def _rope_tab(pos_tokens):
    t = np.asarray(pos_tokens)
    row = (t // GRID_W).astype(np.float32)
    col = (t % GRID_W).astype(np.float32)
    inv = (np.float32(10000.0) ** (-np.arange(16, dtype=np.float32) / np.float32(16))).astype(np.float32)
    tab = np.zeros((64, 2, t.shape[0]), np.float32)
    for i in range(64):
        pos = row if i < 32 else col
        ii = i % 32
        ang = (pos * inv[ii % 16]).astype(np.float32)
        tab[i, 0] = np.cos(ang)
        tab[i, 1] = -np.sin(ang) if ii < 16 else np.sin(ang)
    return np.ascontiguousarray(np.concatenate([tab, tab], axis=0))


def _partner():
    i = np.arange(64)
    ii = i % 32
    return np.where(ii < 16, i + 16, i - 16)


def _na_masks(kind, half):
    c = np.arange(64)
    cs = np.clip(c - 8, 0, 48)
    colv = (c[None, :] >= cs[:, None]) & (c[None, :] < cs[:, None] + 16)
    m = np.zeros((128, 16, 64), np.float32)
    for p in range(128):
        kc = p % 64
        for j in range(16):
            dr = j - 8 + (1 if p >= 64 else 0)
            ok = -7 <= dr <= 7
            if kind == "sample":
                ok = ok and (dr >= -4 if half == 0 else dr <= 3)
            if ok:
                m[p, j, :] = colv[:, kc].astype(np.float32)
    neg = np.where(m > 0, np.float32(0.0), np.float32(NEG)).astype(np.float32)
    return m.reshape(128, 1024), neg.reshape(128, 1024)


def _na_gather(rpb):
    p = np.arange(128)
    kc = p % 64
    j = np.arange(16)
    qc = np.arange(64)
    dr = j[None, :] - 8 + (p[:, None] >= 64)
    dri = np.clip(dr + 7, 0, 14)
    dci = np.clip(kc[:, None] - qc[None, :] + 15, 0, 30)
    g = rpb[:, dri[:, :, None], dci[:, None, :]]
    return np.ascontiguousarray(g.transpose(1, 0, 2, 3).reshape(128, -1)).astype(np.float32)


_CACHE = {}


def kernel(x_prompt, x_sample, mem_prompt, mem_sample, ab_w_in, na_rpb, gqa_q_gain, gqa_k_gain,
           ab_w_out, pool_w, pool_b, pool_scale, ln1_g, ln1_b, xa_wq, xa_wkv, xa_wo, ln2_g, ln2_b,
           ffn_w_up, ffn_conv_w, ffn_conv_b, ffn_w_down, ln3_g, ln3_b, _dbg=None):
    f = lambda a: np.asarray(a, dtype=np.float32)
    x_prompt, x_sample, mem_prompt, mem_sample = f(x_prompt), f(x_sample), f(mem_prompt), f(mem_sample)
    w_in = f(ab_w_in)[0]
    pt = _partner()
    shared = {}
    shared["w_na"] = _kp(w_in[:, 0:1536])
    qcols = w_in[:, 1536:2048].reshape(D, 8, 64)
    kcols = w_in[:, 2048:2176].reshape(D, 2, 64)
    vcols = w_in[:, 2176:2304]
    kdup = np.concatenate([kcols, kcols], axis=2)
    krot = kcols[:, :, pt]
    krdup = np.concatenate([krot, krot], axis=2)
    wg = np.concatenate([qcols.reshape(D, 512), qcols[:, :, pt].reshape(D, 512), kdup.reshape(D, 256),
                         krdup.reshape(D, 256), vcols], axis=1)
    shared["w_gq"] = _kp(wg)
    shared["w_ao"] = _kp(f(ab_w_out)[0])
    for l in range(2):
        shared[f"w_xq{l}"] = _kp(f(xa_wq)[l])
        shared[f"w_xo{l}"] = _kp(f(xa_wo)[l])
        wkv = f(xa_wkv)[l]
        shared[f"w_xkv{l}"] = np.ascontiguousarray(np.stack([_kp(wkv[:, s * 256:(s + 1) * 256]) for s in range(8)]))
        wu = f(ffn_w_up)[l]
        shared[f"w_up{l}"] = np.ascontiguousarray(np.stack(
            [_kp(np.concatenate([wu[:, j * 128:(j + 1) * 128], wu[:, DFF + j * 128:DFF + (j + 1) * 128]], axis=1)) for j in range(NPAIR)]))
        wd = f(ffn_w_down)[l]
        shared[f"w_dn{l}"] = np.ascontiguousarray(np.stack([_kp(wd[:, oc * 128:(oc + 1) * 128]) for oc in range(8)]))
    pw = f(pool_w)[0]
    shared["w_pool"] = np.ascontiguousarray(pw.reshape(4, 2, 128, 256).transpose(2, 0, 1, 3).reshape(128, -1))
    cst = np.zeros((128, 384), np.float32)
    cst[:, 0:128] = np.eye(128, dtype=np.float32)
    cst[0:64, 128:192] = 1.0
    cst[64:128, 192:256] = 1.0
    shared["consts"] = cst
    pv = np.zeros((128, 512), np.float32)
    lns = [(f(ln1_g), f(ln1_b)), (f(ln2_g), f(ln2_b)), (f(ln3_g), f(ln3_b))]
    for l in range(2):
        for n in range(3):
            o = ((l * 3 + n) * 2) * 8
            pv[:, o:o + 8] = lns[n][0][l].reshape(8, 128).T
            pv[:, o + 8:o + 16] = lns[n][1][l].reshape(8, 128).T
        cwl = f(ffn_conv_w)[l]
        pv[:, 96 + l * 132:96 + (l + 1) * 132] = cwl.reshape(3, 44, 128).transpose(2, 1, 0).reshape(128, 132)
        pv[:, 360 + l * 44:360 + (l + 1) * 44] = f(ffn_conv_b)[l].reshape(44, 128).T
    pv[:, 448:456] = f(pool_b)[0].reshape(8, 128).T
    pv[:, 456:464] = f(pool_scale)[0].reshape(8, 128).T
    gq, gk = f(gqa_q_gain)[0], f(gqa_k_gain)[0]
    pv[:, 464] = np.concatenate([gq, gq]); pv[:, 465] = np.concatenate([gq[pt], gq[pt]])
    pv[:, 466] = np.concatenate([gk, gk]); pv[:, 467] = np.concatenate([gk[pt], gk[pt]])
    shared["pvec"] = pv
    shared["na_g"] = _na_gather(f(na_rpb)[0])
    shared["na_m_p"], shared["na_n_p"] = _na_masks("prompt", 0)
    shared["rope_p"] = _rope_tab(np.arange(2048))
    rce = np.ones((128, 8, 16), np.float32)
    for gi, win in enumerate((2, 4, 8, 16)):
        bl, br = win // 2, win - 1 - win // 2
        for e in range(8):
            rce[:, 2 * gi:2 * gi + 2, e] = 1.0 / (min(e, bl) + 1 + br)
            rce[:, 2 * gi:2 * gi + 2, 8 + e] = 1.0 / (bl + 1 + min(7 - e, br))
    shared["rc_edge"] = rce.reshape(128, 128)

    in_maps = []
    for c in range(NCORES):
        m = dict(shared)
        for i in range(2):
            b = 2 * c + i
            m[f"xT_p{i}"] = np.ascontiguousarray(x_prompt[b].T)
            m[f"memT_p{i}"] = _kp(np.ascontiguousarray(mem_prompt[b].T))
        sb, half = c // 2, c % 2
        w0 = 0 if half == 0 else 1920
        e0 = w0 - 256
        tok_e = np.arange(e0, e0 + 2688)
        ex_e = (tok_e >= 0) & (tok_e < 4096)
        tok_r = np.arange(2432, 4096) if half == 0 else np.arange(0, 1664)
        toks = np.concatenate([tok_e, tok_r])
        ex = np.concatenate([ex_e, np.ones(tok_r.shape[0], bool)])
        xs = np.zeros((toks.shape[0], D), np.float32)
        xs[ex] = x_sample[sb][toks[ex]]
        m["xT_s"] = np.ascontiguousarray(xs.T)
        m["memT_s"] = _kp(np.ascontiguousarray(mem_sample[sb].T))
        m["ropeq_s"] = _rope_tab(np.arange(w0, w0 + 2176))
        m["ropek_s"] = _rope_tab(np.clip(toks, 0, 4095))
        m["na_m_s"], m["na_n_s"] = _na_masks("sample", half)
        m["exE_s"] = np.ascontiguousarray(ex_e.astype(np.float32).reshape(21, 128).T)
        m["exK_s"] = np.ascontiguousarray(ex.astype(np.float32).reshape(34, 128).T)
        in_maps.append(m)

    if _dbg is not None and _dbg.startswith(("na", "consts")):
        for m in in_maps:
            for k in list(m.keys()):
                if k.startswith(("w_gq", "w_ao", "w_x", "w_up", "w_dn", "w_pool")):
                    m[k] = np.zeros((2, 2), np.float32)
    key = _dbg
    if key not in _CACHE:
        _CACHE[key] = build_program(_dbg)
    B = _CACHE[key]
    res = run_bass_kernel_spmd(B.nc, in_maps, core_ids=list(range(NCORES)))
    if _dbg:
        return [r["dbg"] for r in res.results]
    y_prompt = np.empty((16, 2048, D), np.float32)
    y_sample = np.empty((4, 4096, D), np.float32)
    for c in range(NCORES):
        r = res.results[c]
        for i in range(2):
            y_prompt[2 * c + i] = r[f"yT_p{i}"].T
        sb, half = c // 2, c % 2
        ys = r["yT_s"]
        if half == 0:
            y_sample[sb, 0:2048] = ys[:, 0:2048].T
        else:
            y_sample[sb, 2048:4096] = ys[:, 128:2176].T
    return (y_prompt, y_sample)
```

```python
import numpy as np
import concourse.bass as bass
import concourse.mybir as mybir
from concourse.bass_utils import run_bass_kernel_spmd

F32 = mybir.dt.float32
BF16 = mybir.dt.bfloat16
ALU = mybir.AluOpType
AF = mybir.ActivationFunctionType

NCORES = 8
D = 1024
DFF = 2816
NPAIR = 22
ALPHA = float((2 * 2) ** 0.25)
LN_EPS = 1e-5
QK_EPS = 1e-6
NEG = -1e30
GRID_W = 64

SMAX = 2176
TBMAX = 440
XSZ = 8 * (SMAX + 16) * 4
OSZ = 8 * SMAX * 2
OFF_X = 0
OFF_O = XSZ
OFF_W = XSZ + OSZ
ARENA_BYTES = 204800
WSZ = ARENA_BYTES - OFF_W


class Tok:
    __slots__ = ("sem", "val", "eng")

    def __init__(self, sem, val, eng):
        self.sem, self.val, self.eng = sem, val, eng


class Buf:
    __slots__ = ("name", "w", "r")

    def __init__(self, name):
        self.name, self.w, self.r = name, None, {}


class Builder:
    EPOCH = 30000

    def __init__(self):
        self.nc = bass.Bass("TRN2", target_bir_lowering=False)
        nc = self.nc
        self.engs = {"pe": nc.tensor, "act": nc.scalar, "dve": nc.vector, "pool": nc.gpsimd, "sp": nc.sync}
        self.esem = {}
        self.ecnt = {}
        self.last = {}
        self.known = {e: {} for e in self.engs}
        self.nsem = 0
        for e in ("pe", "act", "dve", "pool"):
            self._new_epoch(e)
        self.dslots = {q: [[self._sem(), 0, None] for _ in range(n)] for q, n in (("sp", 10), ("pool", 14))}
        self.dnext = {"sp": 0, "pool": 0}
        self.ninst = 0

    def _sem(self):
        self.nsem += 1
        return self.nc.alloc_semaphore(f"s{self.nsem}")

    def _new_epoch(self, e):
        self.esem[e] = self._sem()
        self.ecnt[e] = 0

    def _wait(self, e, t):
        if t is None:
            return
        k = self.known[e]
        key = id(t.sem)
        if k.get(key, 0) >= t.val:
            return
        self.engs[e].wait_ge(t.sem, t.val)
        k[key] = t.val

    def _deps(self, e, reads, writes):
        for b in reads:
            t = b.w
            if t is not None and not (t.eng == e and e == "pe"):
                self._wait(e, t)
        for b in writes:
            t = b.w
            if t is not None and not (t.eng == e and e == "pe"):
                self._wait(e, t)
            for re_, rt in b.r.items():
                if not (re_ == e and e == "pe"):
                    self._wait(e, rt)

    def _commit(self, tok, reads, writes):
        for b in reads:
            b.r[tok.eng] = tok
        for b in writes:
            b.w = tok
            b.r = {}

    def group(self, e, fns, reads=(), writes=()):
        self._deps(e, reads, writes)
        eng = self.engs[e]
        ins = None
        for f in fns:
            ins = f(eng)
        self.ninst += len(fns)
        if self.ecnt[e] >= self.EPOCH:
            self._new_epoch(e)
        self.ecnt[e] += 1
        ins.then_inc(self.esem[e], 1)
        tok = Tok(self.esem[e], self.ecnt[e], e)
        self.last[e] = tok
        self._commit(tok, reads, writes)
        return tok

    def op(self, e, f, reads=(), writes=()):
        return self.group(e, [f], reads, writes)

    def dma(self, q, out, in_, reads=(), writes=()):
        slots = self.dslots[q]
        sl = slots[self.dnext[q] % len(slots)]
        self.dnext[q] += 1
        self._wait(q, sl[2])
        self._deps(q, reads, writes)
        ins = self.engs[q].dma_start(out=out, in_=in_)
        sl[1] += 16
        ins.then_inc(sl[0], 16)
        tok = Tok(sl[0], sl[1], "dma_" + q)
        sl[2] = tok
        self.ninst += 1
        self._commit(tok, reads, writes)
        return tok

    def barrier(self):
        toks = list(self.last.values())
        for q in self.dslots:
            toks += [sl[2] for sl in self.dslots[q] if sl[2] is not None]
        for e in self.engs:
            for t in toks:
                if t.eng != e:
                    self._wait(e, t)


def _blocks(S, nb=5):
    base, rem = divmod(S, nb)
    out, t0 = [], 0
    for i in range(nb):
        T = base + (1 if i < rem else 0)
        out.append((t0, T))
        t0 += T
    return out


def _na_plan(kind):
    plan = []
    if kind == "prompt":
        R = 32
        for r in range(R):
            rs = min(max(r - 4, 0), R - 8)
            lo, hi, je = rs, rs + 7, r
            plan.append(_tiles(lo, hi, je))
    else:
        nQr = 34
        for jq in range(nQr):
            je = jq + 4
            lo, hi = je - 4, je + 3
            if jq < 4:
                hi = 11
            if jq >= 31:
                lo = 30
            plan.append(_tiles(lo, hi, je))
    return plan


def _tiles(lo, hi, je):
    out = []
    for i in range(lo // 2, hi // 2 + 1):
        lov = lo <= 2 * i <= hi
        hiv = lo <= 2 * i + 1 <= hi
        jx = 2 * i - je + 8
        assert 0 <= jx <= 15
        out.append((i, lov, hiv, jx))
    return out


class Seg:
    pass


def build_program(dbg=None):
    B = Builder()
    nc = B.nc

    small = dbg is not None and dbg.startswith(("na", "consts"))

    def din(name, shape):
        if small and name.startswith(("w_gq", "w_ao", "w_x", "w_up", "w_dn", "w_pool")):
            shape = [2, 2]
        return nc.dram_tensor(name, list(shape), F32, kind="ExternalInput").ap()

    def dout(name, shape):
        return nc.dram_tensor(name, list(shape), F32, kind="ExternalOutput").ap()

    segs = []
    for i, (nm, S, nE, nK, qoff, kind) in enumerate(
            [("p0", 2048, 16, 16, 0, "prompt"), ("p1", 2048, 16, 16, 0, "prompt"), ("s", 2176, 21, 34, 256, "sample")]):
        sg = Seg()
        sg.name, sg.S, sg.nE, sg.nK, sg.qoff, sg.kind = nm, S, nE, nK, qoff, kind
        sg.xT = din("xT_" + nm, [D, nK * 128]).rearrange("(k p) t -> p k t", p=128)
        sg.memT = din("memT_" + nm, [128, 8 * 256])
        sg.yT = dout("yT_" + nm, [D, S]).rearrange("(k p) t -> p k t", p=128)
        sg.blocks = _blocks(S)
        sg.plan = _na_plan(kind)
        segs.append(sg)
    rope_p = din("rope_p", [128, 2, 2048])
    ropeq_s = din("ropeq_s", [128, 2, 2176])
    ropek_s = din("ropek_s", [128, 2, 34 * 128])
    na_g = din("na_g", [128, 8 * 1024])
    na_m_p = din("na_m_p", [128, 1024]); na_n_p = din("na_n_p", [128, 1024])
    na_m_s = din("na_m_s", [128, 1024]); na_n_s = din("na_n_s", [128, 1024])
    exE_s = din("exE_s", [128, 21]); exK_s = din("exK_s", [128, 34])
    for sg in segs:
        if sg.kind == "prompt":
            sg.ropeq, sg.ropek, sg.na_m, sg.na_n, sg.exE, sg.exK = rope_p, rope_p, na_m_p, na_n_p, None, None
        else:
            sg.ropeq, sg.ropek, sg.na_m, sg.na_n, sg.exE, sg.exK = ropeq_s, ropek_s, na_m_s, na_n_s, exE_s, exK_s
    w_na = din("w_na", [128, 8 * 1536])
    w_gq = din("w_gq", [128, 8 * 1664])
    w_ao = din("w_ao", [128, 8 * 1024])
    w_xq = [din(f"w_xq{l}", [128, 8 * 1024]) for l in range(2)]
    w_xo = [din(f"w_xo{l}", [128, 8 * 1024]) for l in range(2)]
    w_xkv = [din(f"w_xkv{l}", [8, 128, 8 * 256]) for l in range(2)]
    w_up = [din(f"w_up{l}", [NPAIR, 128, 8 * 256]) for l in range(2)]
    w_dn = [din(f"w_dn{l}", [8, 128, NPAIR * 128]) for l in range(2)]
    w_pool = din("w_pool", [128, 4 * 2 * 256])
    consts = din("consts", [128, 3 * 128])
    pvec = din("pvec", [128, 512])
    rc_edge = din("rc_edge", [128, 8 * 16])
    dbg_out = dout("dbg", [D, SMAX]).rearrange("(k p) t -> p k t", p=128) if dbg else None
    wup_bf = [nc.dram_tensor(f"wupbf{l}", [NPAIR, 128, 8 * 256], BF16).ap() for l in range(2)]
    wdn_bf = [nc.dram_tensor(f"wdnbf{l}", [8, 128, NPAIR * 128], BF16).ap() for l in range(2)]
    b_upc = [[Buf(f"upc{l}_{j}") for j in range(NPAIR)] for l in range(2)]
    b_dnc = [[Buf(f"dnc{l}_{j}") for j in range(8)] for l in range(2)]
    conv_jobs = [("up", l, j) for l in range(2) for j in range(NPAIR)] + [("dn", l, j) for l in range(2) for j in range(8)]
    conv_jobs.sort(key=lambda t: t[1])

    def emit_conv(n):
        for _ in range(n):
            if not conv_jobs or small:
                return
            kind, l, j = conv_jobs.pop(0)
            if kind == "up":
                B.dma("pool", wup_bf[l][j], w_up[l][j], writes=[b_upc[l][j]])
            else:
                B.dma("pool", wdn_bf[l][j], w_dn[l][j], writes=[b_dnc[l][j]])

    ARENA = nc.alloc_sbuf_tensor("arena", [128, ARENA_BYTES // 4], F32)
    PS = nc.alloc_psum_tensor("ps", [128, 8, 512], F32)
    CONB = nc.alloc_sbuf_tensor("conb", [128, 3 * 128], BF16)
    ONESM = nc.alloc_sbuf_tensor("onesm", [128, 128], BF16)
    ONES1 = nc.alloc_sbuf_tensor("ones1", [128, 128], BF16)
    NEGH = nc.alloc_sbuf_tensor("negh", [128, 512], F32)
    NEGB = nc.alloc_sbuf_tensor("negb", [128, 64], BF16)
    PV = nc.alloc_sbuf_tensor("pvec_sb", [128, 512], F32)
    RCE = nc.alloc_sbuf_tensor("rce", [128, 8, 16], F32)
    DER = nc.alloc_sbuf_tensor("der", [128, 8], F32)
    IDENT = CONB[:, 0:128]
    BD = CONB[:, 128:256]

    def view(off, shape, dt):
        esz = 2 if dt == BF16 else 4
        n = int(np.prod(shape[1:]))
        assert off % 4 == 0 and (n * esz) % 4 == 0
        ap = ARENA[:, off // 4: (off + n * esz) // 4]
        if dt == BF16:
            ap = ap.bitcast(BF16)
        if len(shape) == 3:
            ap = ap.rearrange("p (a b) -> p a b", a=shape[1])
        elif len(shape) == 4:
            ap = ap.rearrange("p (a b c) -> p a b c", a=shape[1], b=shape[2])
        return ap

    class Alloc:
        def __init__(self, base, size):
            self.base, self.size, self.off = base, size, 0

        def get(self, shape, dt):
            esz = 2 if dt == BF16 else 4
            n = int(np.prod(shape[1:])) * esz
            n = (n + 3) // 4 * 4
            v = view(self.base + self.off, shape, dt)
            self.off += n
            assert self.off <= self.size, (self.off, self.size)
            return v

    bank_b = [Buf(f"bank{i}") for i in range(8)]
    b_const = Buf("const")

    PV_LN = 0
    PV_CW = 96
    PV_CB = 360
    PV_PB = 448
    PV_PS = 456
    PV_GN = 464

    def ln_gb(l, n, c):
        o = PV_LN + ((l * 3 + n) * 2) * 8
        return PV[:, o + c: o + c + 1], PV[:, o + 8 + c: o + 8 + c + 1]

    B.dma("pool", CONB[:, :], consts[:, :], writes=[b_const])
    B.dma("sp", PV[:, :], pvec[:, :], writes=[b_const])
    B.dma("sp", RCE[:, :, :], rc_edge.rearrange("p (c e) -> p c e", c=8), writes=[b_const])
    B.op("dve", lambda e: e.memset(ONESM[:, :], 1.0 / 1024.0), writes=[b_const])
    B.op("dve", lambda e: e.memset(ONES1[:, :], 1.0), writes=[b_const])
    B.op("dve", lambda e: e.memset(NEGH[:, :], -0.5), writes=[b_const])
    B.op("dve", lambda e: e.memset(NEGB[:, :], NEG), writes=[b_const])
    B.op("dve", lambda e: e.tensor_tensor(out=DER[:, :], in0=PV[:, PV_PB:PV_PB + 8], in1=PV[:, PV_PS:PV_PS + 8],
                                           op=ALU.mult), reads=[b_const], writes=[b_const])
    B.barrier()

    rot = {}

    def rbank(key, banks):
        i = rot.get(key, 0)
        rot[key] = i + 1
        return banks[i % len(banks)]

    def split_dma(q, out2d, in2d, ncols, name, piece=4096):
        bufs = []
        for c0 in range(0, ncols, piece):
            c1 = min(ncols, c0 + piece)
            b = Buf(name)
            B.dma(q, out2d[:, c0:c1], in2d[:, c0:c1], writes=[b])
            bufs.append(b)
        return bufs

    def ln_steps(zw, T, l, n, Xv, xb, t0, zi, banks=(6, 7)):
        Z, ZB, ZSQ, MEAN, VAR, RSTD, b_z, b_zb, b_st = zw
        Z, b_z = Z[zi], b_z[zi]
        bs, bq = banks

        def pe_sum(src, bk):
            B.group("pe", [(lambda e, c=c: e.matmul(PS[:, bk, :T], lhsT=ONESM[:, :], rhs=src[:, c, :T], start=(c == 0), stop=(c == 7)))
                           for c in range(8)], reads=[b_zb, b_const], writes=[bank_b[bk]])

        def s1():
            B.op("act", lambda e: e.activation(out=ZB[:, :, :T], in_=Z[:, :, :T], func=AF.Identity), reads=[b_z], writes=[b_zb])
            B.op("act", lambda e: e.activation(out=ZSQ[:, :, :T], in_=Z[:, :, :T], func=AF.Square), reads=[b_z], writes=[b_zb])
            pe_sum(ZB, bs)
            if bq != bs:
                pe_sum(ZSQ, bq)

        def s2():
            B.op("dve", lambda e: e.tensor_copy(out=MEAN[:, :T], in_=PS[:, bs, :T]), reads=[bank_b[bs]], writes=[b_st])
            B.op("dve", lambda e: e.tensor_tensor(out=VAR[:, :T], in0=MEAN[:, :T], in1=MEAN[:, :T], op=ALU.mult), reads=[b_st], writes=[b_st])
            if bq == bs:
                pe_sum(ZSQ, bq)
            B.op("dve", lambda e: e.tensor_tensor(out=VAR[:, :T], in0=PS[:, bq, :T], in1=VAR[:, :T], op=ALU.subtract),
                 reads=[bank_b[bq], b_st], writes=[b_st])
            B.op("dve", lambda e: e.tensor_scalar(out=VAR[:, :T], in0=VAR[:, :T], scalar1=1.0, scalar2=LN_EPS, op0=ALU.mult, op1=ALU.add),
                 reads=[b_st], writes=[b_st])
            B.op("act", lambda e: e.activation(out=VAR[:, :T], in_=VAR[:, :T], func=AF.Ln), reads=[b_st], writes=[b_st])
            B.op("act", lambda e: e.activation(out=RSTD[:, :T], in_=VAR[:, :T], func=AF.Exp, scale=-0.5), reads=[b_st], writes=[b_st])

        def s3():
            B.op("dve", lambda e: e.tensor_tensor(out=Z[:, :, :T], in0=Z[:, :, :T], in1=MEAN[:, :T].unsqueeze(1).to_broadcast([128, 8, T]),
                                                   op=ALU.subtract), reads=[b_z, b_st], writes=[b_z])
            B.op("dve", lambda e: e.tensor_tensor(out=Z[:, :, :T], in0=Z[:, :, :T], in1=RSTD[:, :T].unsqueeze(1).to_broadcast([128, 8, T]),
                                                   op=ALU.mult), reads=[b_z, b_st], writes=[b_z])

        def s4():
            fns = []
            for c in range(8):
                g, bb = ln_gb(l, n, c)
                fns.append(lambda e, c=c, g=g, bb=bb: e.activation(out=Xv[:, c, 8 + t0: 8 + t0 + T], in_=Z[:, c, :T],
                                                                   func=AF.Identity, bias=bb, scale=g))
            B.group("act", fns, reads=[b_z, b_const], writes=[xb])
        return [s1, s2, s3, s4]

    def ln_block(zw, T, l, n, Xv, xb, t0, zi, extra_reads=()):
        for st in ln_steps(zw, T, l, n, Xv, xb, t0, zi):
            st()

    def zwork(al):
        Z = [al.get([128, 8, TBMAX], F32) for _ in range(2)]
        ZB = al.get([128, 8, TBMAX], BF16)
        ZSQ = al.get([128, 8, TBMAX], BF16)
        MEAN = al.get([128, TBMAX], F32)
        VAR = al.get([128, TBMAX], F32)
        RSTD = al.get([128, TBMAX], F32)
        return (Z, ZB, ZSQ, MEAN, VAR, RSTD, [Buf("z0"), Buf("z1")], Buf("zb"), Buf("zst"))

    out_toks = []
    if dbg == "consts":
        B.dma("sp", dbg_out[:, 0, 0:512], PV[:, :], reads=[b_const])
        segs_run = []
    else:
        segs_run = segs
    for sg in segs_run:
        S, nE, nK, qoff, blocks = sg.S, sg.nE, sg.nK, sg.qoff, sg.blocks
        nQr = S // 64
        Xv = view(OFF_X, [128, 8, S + 16], F32)
        OT = view(OFF_O, [128, 8, S], BF16)
        b_x = [Buf(f"x{i}") for i in range(len(blocks))]
        b_xpad = Buf("xpad")
        b_ot = Buf("ot")

        ax = Alloc(OFF_X, XSZ)
        aw = Alloc(OFF_W, WSZ)
        VAN = ax.get([128, nE, 8, 128], BF16)
        QA = ax.get([128, 4, S], BF16)
        PTn = [ax.get([128, 2, 384], BF16) for _ in range(2)]
        SF = [ax.get([128, 2, 384], F32) for _ in range(2)]
        Wna = aw.get([128, 8, 1536], BF16)
        BT = aw.get([128, 8, 16, 64], BF16)
        XS = [aw.get([128, 8, 512], BF16) for _ in range(2)]
        KA = aw.get([128, 4, nE * 128], BF16)
        RECn = aw.get([128, 512], F32)
        MK01 = aw.get([128, 1024], F32)
        NEGT = aw.get([128, 1024], F32)
        EXT = aw.get([128, 64], F32)
        PTn.append(aw.get([128, 2, 384], BF16))
        SF.append(aw.get([128, 2, 384], F32))
        GT = view(OFF_O, [128, 8, 1024], F32)
        b_ka, b_va, b_qa, b_w, b_bt, b_rec = (Buf(n) for n in "ka va qa w bt rec".split())
        b_xs = [Buf("xs0"), Buf("xs1")]
        b_pt = [Buf("pt0"), Buf("pt1"), Buf("pt2")]
        b_sf = [Buf("sf0"), Buf("sf1"), Buf("sf2")]
        b_gt = b_ot

        bw_l = split_dma("pool", Wna.rearrange("p k n -> p (k n)"), w_na, 8 * 1536, "wna")
        for q4 in range(4):
            B.dma("sp", GT[:, 2 * q4:2 * q4 + 2, :].rearrange("p h n -> p (h n)"), na_g[:, q4 * 2048:(q4 + 1) * 2048], writes=[b_gt])
        B.dma("sp", MK01[:, :], sg.na_m[:, :], writes=[b_gt])
        B.dma("sp", NEGT[:, :], sg.na_n[:, :], writes=[b_gt])
        B.op("dve", lambda e: e.tensor_tensor(out=GT[:, :, :], in0=GT[:, :, :], in1=MK01[:, :].unsqueeze(1).to_broadcast([128, 8, 1024]),
                                               op=ALU.mult), reads=[b_gt], writes=[b_gt])
        B.op("dve", lambda e: e.tensor_tensor(out=BT.rearrange("p h j q -> p h (j q)"), in0=GT[:, :, :],
                                               in1=NEGT[:, :].unsqueeze(1).to_broadcast([128, 8, 1024]), op=ALU.add),
             reads=[b_gt], writes=[b_bt])
        if sg.exE is None:
            B.op("dve", lambda e: e.memset(VAN[:, :, :, 64:128], 1.0), writes=[b_va])
        else:
            B.dma("sp", EXT[:, :nE], sg.exE[:, :], writes=[b_gt])
            for hq in range(8):
                B.op("dve", lambda e, hq=hq: e.tensor_copy(out=VAN[:, :, hq, 64:128], in_=EXT[:, :nE].unsqueeze(2).to_broadcast([128, nE, 64])),
                     reads=[b_gt], writes=[b_va])

        nxs = [0]

        def load_xs(c0, w):
            i = nxs[0] % 2
            nxs[0] += 1
            B.dma("pool", XS[i][:, :, :w], sg.xT[:, :, c0:c0 + w], writes=[b_xs[i]])
            return XS[i], b_xs[i]

        for c0 in range(0, nE * 128, 512):
            w = min(512, nE * 128 - c0)
            xs, bxs = load_xs(c0, w)
            for c in range(4):
                bk = rbank("a", [0, 1, 2, 3])
                B.group("pe", [(lambda e, k=k, c=c, bk=bk: e.matmul(PS[:, bk, :w], lhsT=Wna[:, k, 512 + c * 128: 512 + (c + 1) * 128],
                                                                    rhs=xs[:, k, :w], start=(k == 0), stop=(k == 7))) for k in range(8)],
                        reads=bw_l + [bxs], writes=[bank_b[bk]])
                B.op("act", lambda e, c=c, bk=bk: e.activation(out=KA[:, c, c0:c0 + w], in_=PS[:, bk, :w], func=AF.Identity),
                     reads=[bank_b[bk]], writes=[b_ka])
            for tt in range(w // 128):
                bk = rbank("a", [0, 1, 2, 3])
                B.group("pe", [(lambda e, k=k, tt=tt, bk=bk: e.matmul(PS[:, bk, :], lhsT=xs[:, k, tt * 128:(tt + 1) * 128],
                                                                      rhs=Wna[:, k, 1024:1536], start=(k == 0), stop=(k == 7))) for k in range(8)],
                        reads=bw_l + [bxs], writes=[bank_b[bk]])
                B.op("dve", lambda e, tt=tt, bk=bk: e.tensor_copy(out=VAN[:, c0 // 128 + tt, :, 0:64],
                                                                  in_=PS[:, bk, :].rearrange("p (h d) -> p h d", h=8)),
                     reads=[bank_b[bk]], writes=[b_va])
        for (t0, T) in blocks:
            xs, bxs = load_xs(qoff + t0, T)
            for c in range(4):
                bk = rbank("a", [0, 1, 2, 3])
                B.group("pe", [(lambda e, k=k, c=c, bk=bk: e.matmul(PS[:, bk, :T], lhsT=Wna[:, k, c * 128:(c + 1) * 128],
                                                                    rhs=xs[:, k, :T], start=(k == 0), stop=(k == 7))) for k in range(8)],
                        reads=bw_l + [bxs], writes=[bank_b[bk]])
                B.op("act", lambda e, c=c, bk=bk: e.activation(out=QA[:, c, t0:t0 + T], in_=PS[:, bk, :T], func=AF.Identity, scale=0.125),
                     reads=[bank_b[bk]], writes=[b_qa])

        if dbg == "na_kv" and sg is segs[DBG_SEG]:
            for c in range(4):
                B.dma("pool", dbg_out[:, c, :S], QA[:, c, :], reads=[b_qa])
                B.dma("pool", dbg_out[:, 4 + c, :S], KA[:, c, :S], reads=[b_ka])
            break
        def na_scores(jq, hp):
            tiles = sg.plan[jq]
            q0 = jq * 64
            sb = rbank("nas", [0, 2, 6])
            nt = len(tiles)
            fns = []
            for hh in range(2):
                h, c, bp = 2 * hp + hh, hp, hh * 64
                I64 = IDENT[bp:bp + 64, bp:bp + 64]
                for ti, (i, lov, hiv, jx) in enumerate(tiles):
                    if lov and hiv:
                        p0, p1, k0, kn, fill = 0, 128, i * 128, 128, None
                    elif lov:
                        p0, p1, k0, kn, fill = 0, 64, i * 128, 64, (64, 128)
                    else:
                        p0, p1, k0, kn, fill = 64, 128, i * 128 + 64, 64, (0, 64)
                    o = PS[p0:p1, sb + hh, ti * 64:(ti + 1) * 64]
                    fns.append(lambda e, o=o, c=c, bp=bp, k0=k0, kn=kn: e.matmul(
                        o, lhsT=KA[bp:bp + 64, c, k0:k0 + kn], rhs=QA[bp:bp + 64, c, q0:q0 + 64], start=True, stop=True))
                    if fill is not None:
                        of = PS[fill[0]:fill[1], sb + hh, ti * 64:(ti + 1) * 64]
                        fns.append(lambda e, of=of, bp=bp, I64=I64: e.matmul(
                            of, lhsT=NEGB[bp:bp + 64, 0:64], rhs=I64, start=True, stop=True))
            B.group("pe", fns, reads=[b_ka, b_qa, b_const], writes=[bank_b[sb], bank_b[sb + 1]])
            return (jq, hp, sb, nt, tiles)

        def na_exp(st):
            jq, hp, sb, nt, tiles = st
            pi = rbank("napt", [0, 1, 2])
            jx0 = tiles[0][3]
            for hh in range(2):
                h = 2 * hp + hh
                B.op("dve", lambda e, hh=hh, h=h: e.tensor_tensor(
                    out=SF[pi][:, hh, :nt * 64].rearrange("p (t q) -> p t q", q=64), in0=PS[:, sb + hh, :nt * 64].rearrange("p (t q) -> p t q", q=64),
                    in1=BT[:, h, jx0:jx0 + 2 * nt - 1:2, :], op=ALU.add),
                    reads=[bank_b[sb + hh], b_bt], writes=[b_sf[pi]])
            B.op("act", lambda e: e.activation(out=PTn[pi][:, :, :nt * 64], in_=SF[pi][:, :, :nt * 64], func=AF.Exp),
                 reads=[b_sf[pi]], writes=[b_pt[pi]])
            return (jq, hp, pi)

        def na_pv(item, ob):
            jq, hp, pi = item
            tiles = sg.plan[jq]
            nt = len(tiles)
            fns = []
            for hh in range(2):
                h = 2 * hp + hh
                for ti, (i, lov, hiv, jx) in enumerate(tiles):
                    rhs = PTn[pi][:, hh, ti * 64:(ti + 1) * 64]
                    fns.append(lambda e, rhs=rhs, i=i, h=h, ti=ti: e.matmul(
                        PS[:, ob, h * 64:(h + 1) * 64], lhsT=VAN[:, i, h, :], rhs=rhs, start=(ti == 0), stop=(ti == nt - 1)))
            B.group("pe", fns, reads=[b_va, b_pt[pi]], writes=[bank_b[ob]])

        def na_norm(jq, ob):
            q0 = jq * 64
            B.op("act", lambda e: e.activation(out=RECn[0:64, :], in_=PS[64:128, ob, :], func=AF.Ln), reads=[bank_b[ob]], writes=[b_rec])
            B.op("act", lambda e: e.activation(out=RECn[0:64, :], in_=RECn[0:64, :], func=AF.Exp, scale=-1.0), reads=[b_rec], writes=[b_rec])
            pv = PS[0:64, ob, :].rearrange("p (c hh q) -> p c hh q", hh=2, q=64)
            rv = RECn[0:64, :].rearrange("p (c hh q) -> p c hh q", hh=2, q=64)
            B.op("dve", lambda e: e.tensor_tensor(out=OT[0:64, 0:4, q0:q0 + 64], in0=pv[:, :, 0, :], in1=rv[:, :, 0, :], op=ALU.mult),
                 reads=[bank_b[ob], b_rec], writes=[b_ot])
            B.op("dve", lambda e: e.tensor_tensor(out=OT[64:128, 0:4, q0:q0 + 64], in0=pv[:, :, 1, :], in1=rv[:, :, 1, :], op=ALU.mult),
                 reads=[bank_b[ob], b_rec], writes=[b_ot])

        pendq = []

        def na_retire():
            it = pendq.pop(0)
            na_pv(it, 4 + (it[0] % 2))
            if it[1] == 3:
                na_norm(it[0], 4 + (it[0] % 2))

        for jq in range(nQr):
            for hp2 in range(2):
                if conv_jobs and conv_jobs[0][1] == 0:
                    emit_conv(1)
                sa = na_scores(jq, 2 * hp2)
                sb_ = na_scores(jq, 2 * hp2 + 1)
                pendq.append(na_exp(sa))
                if len(pendq) > 2:
                    na_retire()
                pendq.append(na_exp(sb_))
                if len(pendq) > 2:
                    na_retire()
        while pendq:
            na_retire()
        B.barrier()
        if dbg == "na" and sg is segs[DBG_SEG]:
            for c in range(4):
                B.dma("pool", dbg_out[:, c, :S], OT[:, c, :], reads=[b_ot])
            break

        ax = Alloc(OFF_X, XSZ)
        aw = Alloc(OFF_W, WSZ)
        QB = ax.get([128, 4, S], BF16)
        KB = ax.get([128, 2, nK * 128], BF16)
        VAUG = ax.get([128, nK, 2, 128], BF16)
        PTg = [ax.get([128, 2, 512], BF16) for _ in range(3)]
        SQ_ = [ax.get([128, 512], BF16)]
        RR_ = [ax.get([128, 512], F32)]
        T1_ = [ax.get([128, 512], F32)]
        T2_ = [ax.get([128, 512], F32)]
        Wg = aw.get([128, 8, 1664], BF16)
        XS = [aw.get([128, 8, 512], BF16) for _ in range(2)]
        CS = [aw.get([128, 2, 512], F32) for _ in range(2)]
        CG = aw.get([128, 512], F32)
        SG = aw.get([128, 512], F32)
        RECg = aw.get([128, 512], F32)
        EXT = aw.get([128, 64], F32)
        SQ_.append(aw.get([128, 512], BF16)); RR_.append(aw.get([128, 512], F32))
        T1_.append(aw.get([128, 512], F32)); T2_.append(aw.get([128, 512], F32))
        b_qb, b_kb, b_vaug, b_w, b_cg, b_tmp, b_rec = (Buf(n) for n in "qb kb vaug wg cg tmp recg".split())
        b_tmp2 = [Buf("tmpA"), Buf("tmpB")]
        qkn = [0]
        b_xs = [Buf("xs0"), Buf("xs1")]
        b_cs = [Buf("cs0"), Buf("cs1")]
        b_pt = [Buf("ptg0"), Buf("ptg1"), Buf("ptg2")]
        bw_l = split_dma("pool", Wg.rearrange("p k n -> p (k n)"), w_gq, 8 * 1664, "wg")
        if sg.exK is None:
            B.op("dve", lambda e: e.memset(VAUG[:, :, :, 64:128], 1.0), writes=[b_vaug])
        else:
            B.dma("sp", EXT[:, :nK], sg.exK[:, :], writes=[b_tmp])
            B.op("dve", lambda e: e.tensor_copy(out=VAUG[:, :, 0, 64:128], in_=EXT[:, :nK].unsqueeze(2).to_broadcast([128, nK, 64])),
                 reads=[b_tmp], writes=[b_vaug])
            B.op("dve", lambda e: e.tensor_copy(out=VAUG[:, :, 1, 64:128], in_=EXT[:, :nK].unsqueeze(2).to_broadcast([128, nK, 64])),
                 reads=[b_tmp], writes=[b_vaug])
        QOFF, QROFF, KOFF, KROFF, VOFF = 0, 512, 1024, 1280, 1536
        ncs = [0]

        def load_cs(tab, c0, w, gcol):
            i = ncs[0] % 2
            ncs[0] += 1
            B.dma("sp", CS[i][:, :, :w], tab[:, :, c0:c0 + w], writes=[b_cs[i]])
            B.op("dve", lambda e: e.tensor_scalar(out=CG[:, :w], in0=CS[i][:, 0, :w], scalar1=PV[:, PV_GN + gcol:PV_GN + gcol + 1],
                                                   scalar2=None, op0=ALU.mult), reads=[b_cs[i], b_const], writes=[b_cg])
            B.op("dve", lambda e: e.tensor_scalar(out=SG[:, :w], in0=CS[i][:, 1, :w], scalar1=PV[:, PV_GN + gcol + 1:PV_GN + gcol + 2],
                                                   scalar2=None, op0=ALU.mult), reads=[b_cs[i], b_const], writes=[b_cg])

        def qk_proj(xs, bxs, w, col, rcol, out_ap, b_out, scale):
            par = qkn[0] % 2
            qkn[0] += 1
            SQ, RR, T1, T2, b_tmp = SQ_[par], RR_[par], T1_[par], T2_[par], b_tmp2[par]
            ba = rbank("ga", [0, 1])
            bb = rbank("gb", [2, 3])
            bc = rbank("gc", [4, 5])
            B.group("pe", [(lambda e, k=k: e.matmul(PS[:, ba, :w], lhsT=Wg[:, k, col:col + 128], rhs=xs[:, k, :w],
                                                     start=(k == 0), stop=(k == 7))) for k in range(8)],
                    reads=bw_l + [bxs], writes=[bank_b[ba]])
            B.group("pe", [(lambda e, k=k: e.matmul(PS[:, bb, :w], lhsT=Wg[:, k, rcol:rcol + 128], rhs=xs[:, k, :w],
                                                     start=(k == 0), stop=(k == 7))) for k in range(8)],
                    reads=bw_l + [bxs], writes=[bank_b[bb]])
            B.op("act", lambda e: e.activation(out=SQ[:, :w], in_=PS[:, ba, :w], func=AF.Square), reads=[bank_b[ba]], writes=[b_tmp])
            B.op("pe", lambda e: e.matmul(PS[:, bc, :w], lhsT=BD, rhs=SQ[:, :w], start=True, stop=True),
                 reads=[b_tmp, b_const], writes=[bank_b[bc]])
            B.op("dve", lambda e: e.tensor_scalar(out=RR[:, :w], in0=PS[:, bc, :w], scalar1=1.0 / 64.0, scalar2=QK_EPS,
                                                   op0=ALU.mult, op1=ALU.add), reads=[bank_b[bc]], writes=[b_tmp])
            B.op("act", lambda e: e.activation(out=RR[:, :w], in_=RR[:, :w], func=AF.Ln), reads=[b_tmp], writes=[b_tmp])
            B.op("act", lambda e: e.activation(out=RR[:, :w], in_=RR[:, :w], func=AF.Exp, scale=-0.5), reads=[b_tmp], writes=[b_tmp])
            B.op("dve", lambda e: e.tensor_tensor(out=T1[:, :w], in0=PS[:, ba, :w], in1=CG[:, :w], op=ALU.mult),
                 reads=[bank_b[ba], b_cg], writes=[b_tmp])
            B.op("dve", lambda e: e.tensor_tensor(out=T2[:, :w], in0=PS[:, bb, :w], in1=SG[:, :w], op=ALU.mult),
                 reads=[bank_b[bb], b_cg], writes=[b_tmp])
            B.op("dve", lambda e: e.tensor_tensor(out=T1[:, :w], in0=T1[:, :w], in1=T2[:, :w], op=ALU.add), reads=[b_tmp], writes=[b_tmp])
            B.op("dve", lambda e: e.scalar_tensor_tensor(out=out_ap, in0=T1[:, :w], scalar=scale, in1=RR[:, :w],
                                                          op0=ALU.mult, op1=ALU.mult), reads=[b_tmp], writes=[b_out])

        nxs[0] = 0
        for c0 in range(0, nK * 128, 512):
            w = min(512, nK * 128 - c0)
            xs, bxs = load_xs(c0, w)
            load_cs(sg.ropek, c0, w, 2)
            for g in range(2):
                qk_proj(xs, bxs, w, KOFF + g * 128, KROFF + g * 128, KB[:, g, c0:c0 + w], b_kb, 1.0)
            for tt in range(w // 128):
                bk = rbank("gv", [6, 7])
                B.group("pe", [(lambda e, k=k, tt=tt, bk=bk: e.matmul(PS[:, bk, 0:128], lhsT=xs[:, k, tt * 128:(tt + 1) * 128],
                                                                      rhs=Wg[:, k, VOFF:VOFF + 128], start=(k == 0), stop=(k == 7))) for k in range(8)],
                        reads=bw_l + [bxs], writes=[bank_b[bk]])
                B.op("act", lambda e, tt=tt, bk=bk: e.activation(out=VAUG[:, c0 // 128 + tt, :, 0:64],
                                                                 in_=PS[:, bk, 0:128].rearrange("p (g d) -> p g d", g=2), func=AF.Identity),
                     reads=[bank_b[bk]], writes=[b_vaug])
        for (t0, T) in blocks:
            xs, bxs = load_xs(qoff + t0, T)
            load_cs(sg.ropeq, t0, T, 0)
            for c in range(4):
                qk_proj(xs, bxs, T, QOFF + c * 128, QROFF + c * 128, QB[:, c, t0:t0 + T], b_qb, 0.125)

        def gq_scores(t0, T, c, kg):
            g = c // 2
            sb = rbank("gqs", [0, 2])
            B.group("pe", [(lambda e, hh=hh: e.matmul(PS[:, sb + hh, :T], lhsT=KB[hh * 64:(hh + 1) * 64, g, kg * 128:(kg + 1) * 128],
                                                       rhs=QB[hh * 64:(hh + 1) * 64, c, t0:t0 + T], start=True, stop=True)) for hh in range(2)],
                    reads=[b_kb, b_qb], writes=[bank_b[sb], bank_b[sb + 1]])
            pi = rbank("gqpt", [0, 1, 2])
            B.op("act", lambda e: e.activation(out=PTg[pi][:, :, :T], in_=PS[:, sb:sb + 2, :T], func=AF.Exp),
                 reads=[bank_b[sb], bank_b[sb + 1]], writes=[b_pt[pi]])
            return (t0, T, c, kg, pi)

        def gq_pv(item):
            t0, T, c, kg, pi = item
            g = c // 2
            ob = 4 + 2 * (c % 2)
            B.group("pe", [(lambda e, hh=hh: e.matmul(PS[:, ob + hh, :T], lhsT=VAUG[:, kg, g, :], rhs=PTg[pi][:, hh, :T],
                                                       start=(kg == 0), stop=(kg == nK - 1))) for hh in range(2)],
                    reads=[b_vaug, b_pt[pi]], writes=[bank_b[ob], bank_b[ob + 1]])
            if kg == nK - 1:
                for hh in range(2):
                    bp = hh * 64
                    B.op("dve", lambda e, hh=hh: e.reciprocal(out=RECg[0:64, :T], in_=PS[64:128, ob + hh, :T]), reads=[bank_b[ob + hh]], writes=[b_rec])
                    B.op("dve", lambda e, hh=hh, bp=bp: e.tensor_tensor(out=OT[bp:bp + 64, 4 + c, t0:t0 + T], in0=PS[0:64, ob + hh, :T],
                                                                        in1=RECg[0:64, :T], op=ALU.mult),
                         reads=[bank_b[ob + hh], b_rec], writes=[b_ot])

        pend = []
        for (t0, T) in blocks:
            for c in range(4):
                for kg in range(nK):
                    if kg % 4 == 0:
                        emit_conv(1)
                    pend.append(gq_scores(t0, T, c, kg))
                    if len(pend) > 1:
                        gq_pv(pend.pop(0))
        while pend:
            gq_pv(pend.pop(0))
        B.barrier()

        if dbg == "attn" and sg is segs[DBG_SEG]:
            for c in range(8):
                B.dma("pool", dbg_out[:, c, :S], OT[:, c, :], reads=[b_ot])
            break

        def stage_xattn(l):
            ao = Alloc(OFF_O, OSZ)
            aw = Alloc(OFF_W, WSZ)
            Wq = ao.get([128, 8, 1024], BF16)
            Wo = ao.get([128, 8, 1024], BF16)
            zw = zwork(aw)
            MT = aw.get([128, 8, 256], BF16)
            MK = aw.get([128, 8, 256], BF16)
            MV = aw.get([128, 2, 1024], BF16)
            SL = [aw.get([128, 8, 256], BF16) for _ in range(2)]
            XBb = aw.get([128, 8, TBMAX], BF16)
            QX = aw.get([128, 8, TBMAX], BF16)
            OX = aw.get([128, 8, TBMAX], BF16)
            PTx = [aw.get([128, 2, TBMAX], BF16) for _ in range(2)]
            REC = aw.get([128, TBMAX], F32)
            b_wq, b_wo, b_mt, b_mk, b_mv, b_xbb, b_qx, b_ox, b_rec = (Buf(n) for n in "wq wo mt mk mv xbb qx ox rec".split())
            b_sl = [Buf("sl0"), Buf("sl1")]
            b_pt = [Buf("ptx0"), Buf("ptx1")]
            bwq_l, bwo_l = [], []
            B.dma("pool", MT.rearrange("p k n -> p (k n)"), sg.memT[:, :], writes=[b_mt])
            for s in range(8):
                B.dma("pool", SL[s % 2].rearrange("p k n -> p (k n)"), w_xkv[l][s], writes=[b_sl[s % 2]])
                if s == 0:
                    bwq_l.extend(split_dma("pool", Wq.rearrange("p k n -> p (k n)"), w_xq[l], 8192, "wq"))
                if s == 1:
                    bwo_l.extend(split_dma("pool", Wo.rearrange("p k n -> p (k n)"), w_xo[l], 8192, "wo"))
                sl = SL[s % 2]
                if s < 4:
                    for cc in range(2):
                        bk = rbank("x", [0, 1])
                        B.group("pe", [(lambda e, k=k, cc=cc, bk=bk: e.matmul(PS[:, bk, :256], lhsT=sl[:, k, cc * 128:(cc + 1) * 128],
                                                                              rhs=MT[:, k, :], start=(k == 0), stop=(k == 7))) for k in range(8)],
                                reads=[b_sl[s % 2], b_mt], writes=[bank_b[bk]])
                        B.op("act", lambda e, cc=cc, bk=bk: e.activation(out=MK[:, 2 * s + cc, :], in_=PS[:, bk, :256], func=AF.Identity),
                             reads=[bank_b[bk]], writes=[b_mk])
                else:
                    for mt in range(2):
                        bk = rbank("x", [0, 1])
                        B.group("pe", [(lambda e, k=k, mt=mt, bk=bk: e.matmul(PS[:, bk, :256], lhsT=MT[:, k, mt * 128:(mt + 1) * 128],
                                                                              rhs=sl[:, k, :], start=(k == 0), stop=(k == 7))) for k in range(8)],
                                reads=[b_sl[s % 2], b_mt], writes=[bank_b[bk]])
                        B.op("act", lambda e, mt=mt, bk=bk: e.activation(out=MV[:, mt, (s - 4) * 256:(s - 3) * 256], in_=PS[:, bk, :256],
                                                                         func=AF.Identity), reads=[bank_b[bk]], writes=[b_mv])
            def qphase(bi):
                t0, T = blocks[bi]
                B.op("act", lambda e: e.activation(out=XBb[:, :, :T], in_=Xv[:, :, 8 + t0:8 + t0 + T], func=AF.Identity),
                     reads=[b_x[bi]], writes=[b_xbb])
                for ch in range(8):
                    bk = rbank("x", [0, 1])
                    B.group("pe", [(lambda e, k=k, ch=ch, bk=bk: e.matmul(PS[:, bk, :T], lhsT=Wq[:, k, ch * 128:(ch + 1) * 128],
                                                                          rhs=XBb[:, k, :T], start=(k == 0), stop=(k == 7))) for k in range(8)],
                            reads=bwq_l + [b_xbb], writes=[bank_b[bk]])
                    B.op("act", lambda e, ch=ch, bk=bk: e.activation(out=QX[:, ch, :T], in_=PS[:, bk, :T], func=AF.Identity, scale=1.0 / 16.0),
                         reads=[bank_b[bk]], writes=[b_qx])

            qphase(0)
            pend_ln = []
            for bi, (t0, T) in enumerate(blocks):
                Z, b_z = zw[0][bi % 2], zw[6][bi % 2]
                def xa_scores(h):
                    sb = 0 if h % 2 == 0 else 2
                    fns = []
                    for mt in range(2):
                        for dc in range(2):
                            fns.append(lambda e, mt=mt, dc=dc: e.matmul(PS[:, sb + mt, :T], lhsT=MK[:, 2 * h + dc, mt * 128:(mt + 1) * 128],
                                                                        rhs=QX[:, 2 * h + dc, :T], start=(dc == 0), stop=(dc == 1)))
                    B.group("pe", fns, reads=[b_mk, b_qx], writes=[bank_b[sb], bank_b[sb + 1]])
                    pi = h % 2
                    B.op("act", lambda e: e.activation(out=PTx[pi][:, :, :T], in_=PS[:, sb:sb + 2, :T], func=AF.Exp),
                         reads=[bank_b[sb], bank_b[sb + 1]], writes=[b_pt[pi]])

                def xa_pv(h):
                    pi = h % 2
                    fns = []
                    for dc in range(2):
                        for mt in range(2):
                            fns.append(lambda e, mt=mt, dc=dc: e.matmul(PS[:, 4 + dc, :T], lhsT=MV[:, mt, h * 256 + dc * 128:h * 256 + (dc + 1) * 128],
                                                                        rhs=PTx[pi][:, mt, :T], start=(mt == 0), stop=(mt == 1)))
                    B.group("pe", fns, reads=[b_mv, b_pt[pi]], writes=[bank_b[4], bank_b[5]])
                    B.group("pe", [(lambda e, mt=mt: e.matmul(PS[:, 6, :T], lhsT=ONES1[:, :], rhs=PTx[pi][:, mt, :T],
                                                               start=(mt == 0), stop=(mt == 1))) for mt in range(2)],
                            reads=[b_const, b_pt[pi]], writes=[bank_b[6]])
                    B.op("act", lambda e: e.activation(out=REC[:, :T], in_=PS[:, 6, :T], func=AF.Ln), reads=[bank_b[6]], writes=[b_rec])
                    B.op("act", lambda e: e.activation(out=REC[:, :T], in_=REC[:, :T], func=AF.Exp, scale=-1.0), reads=[b_rec], writes=[b_rec])
                    B.op("dve", lambda e: e.tensor_tensor(out=OX[:, 2 * h:2 * h + 2, :T], in0=PS[:, 4:6, :T],
                                                           in1=REC[:, :T].unsqueeze(1).to_broadcast([128, 2, T]), op=ALU.mult),
                         reads=[bank_b[4], bank_b[5], b_rec], writes=[b_ox])

                pend_h = None
                for h in range(4):
                    xa_scores(h)
                    if pend_h is not None:
                        xa_pv(pend_h)
                    pend_h = h
                    if pend_ln:
                        pend_ln.pop(0)()
                xa_pv(pend_h)
                while pend_ln:
                    pend_ln.pop(0)()
                for oc in range(8):
                    bk = rbank("x", [0, 1])
                    B.group("pe", [(lambda e, k=k, oc=oc, bk=bk: e.matmul(PS[:, bk, :T], lhsT=Wo[:, k, oc * 128:(oc + 1) * 128],
                                                                          rhs=OX[:, k, :T], start=(k == 0), stop=(k == 7))) for k in range(8)],
                            reads=bwo_l + [b_ox], writes=[bank_b[bk]])
                    B.op("dve", lambda e, oc=oc, bk=bk: e.scalar_tensor_tensor(out=Z[:, oc, :T], in0=Xv[:, oc, 8 + t0:8 + t0 + T], scalar=ALPHA,
                                                                               in1=PS[:, bk, :T], op0=ALU.mult, op1=ALU.add),
                         reads=[b_x[bi], bank_b[bk]], writes=[b_z])
                if bi + 1 < len(blocks):
                    qphase(bi + 1)
                pend_ln = ln_steps(zw, T, l, 1, Xv, b_x[bi], t0, bi % 2, banks=(7, 7))
            while pend_ln:
                pend_ln.pop(0)()
            B.barrier()

        def stage_ffn(l, final):
            emit_conv(1000)
            ao = Alloc(OFF_O, OSZ)
            aw = Alloc(OFF_W, WSZ)
            GTt = ao.get([128, NPAIR, TBMAX], BF16)
            zw = zwork(aw)
            XBf = [aw.get([128, 8, TBMAX + 2], BF16) for _ in range(2)]
            UP = [aw.get([128, 8, 256], BF16) for _ in range(3)] + [ao.get([128, 8, 256], BF16)]
            DN = [aw.get([128, NPAIR, 128], BF16) for _ in range(2)] + [ao.get([128, NPAIR, 128], BF16) for _ in range(2)]
            CV = [aw.get([128, TBMAX], F32) for _ in range(3)]
            CGt = [aw.get([128, TBMAX], F32) for _ in range(2)]
            GL = [aw.get([128, TBMAX], F32) for _ in range(2)]
            HAL = aw.get([128, 8, 2 * 8], BF16)
            b_hal = Buf("hal")
            b_gt, b_xbf = Buf("gtt"), [Buf("xbf0"), Buf("xbf1")]
            b_up = [Buf(f"up{i}") for i in range(4)]
            b_dn = [Buf(f"dn{i}") for i in range(4)]
            b_cv = [Buf("cv0"), Buf("cv1"), Buf("cv2")]
            b_cg2 = [Buf("cg0"), Buf("cg1")]
            b_gl = [Buf("gl0"), Buf("gl1")]
            nblk = len(blocks)
            up_seq = [(bi, j) for bi in range(nblk) for j in range(NPAIR)]
            dn_seq = [(bi, oc) for bi in range(nblk) for oc in range(8)]
            up_iss, dn_iss = [0], [0]

            def up_prefetch(upto):
                while up_iss[0] <= min(upto, len(up_seq) - 1):
                    i = up_iss[0]
                    B.dma("sp", UP[i % 4].rearrange("p k n -> p (k n)"), wup_bf[l][up_seq[i][1]], reads=[b_upc[l][up_seq[i][1]]], writes=[b_up[i % 4]])
                    up_iss[0] += 1

            def dn_prefetch(upto):
                while dn_iss[0] <= min(upto, len(dn_seq) - 1):
                    i = dn_iss[0]
                    B.dma("sp", DN[i % 4].rearrange("p k n -> p (k n)"), wdn_bf[l][dn_seq[i][1]], reads=[b_dnc[l][dn_seq[i][1]]], writes=[b_dn[i % 4]])
                    dn_iss[0] += 1

            def cw(ch, tap):
                o = PV_CW + (l * 44 + ch) * 3 + tap
                return PV[:, o:o + 1]

            def cb(ch):
                o = PV_CB + l * 44 + ch
                return PV[:, o:o + 1]

            def conv(bk, ch, dst, bdst, T):
                B.op("act", lambda e: e.activation(out=dst[:, :T], in_=PS[:, bk, 1:T + 1], func=AF.Identity, bias=cb(ch), scale=cw(ch, 1)),
                     reads=[bank_b[bk], b_const], writes=[bdst])
                B.op("dve", lambda e: e.scalar_tensor_tensor(out=dst[:, :T], in0=PS[:, bk, 0:T], scalar=cw(ch, 0), in1=dst[:, :T],
                                                              op0=ALU.mult, op1=ALU.add), reads=[bank_b[bk], bdst, b_const], writes=[bdst])
                B.op("dve", lambda e: e.scalar_tensor_tensor(out=dst[:, :T], in0=PS[:, bk, 2:T + 2], scalar=cw(ch, 2), in1=dst[:, :T],
                                                              op0=ALU.mult, op1=ALU.add), reads=[bank_b[bk], bdst, b_const], writes=[bdst])

            for bi, (t0, T) in enumerate(blocks):
                rd = [b_xpad] + ([b_x[bi - 1]] if bi > 0 else []) + ([b_x[bi + 1]] if bi + 1 < nblk else [])
                B.op("act", lambda e: e.activation(out=HAL[:, :, 2 * bi:2 * bi + 1], in_=Xv[:, :, 8 + t0 - 1:8 + t0], func=AF.Identity),
                     reads=rd, writes=[b_hal])
                B.op("act", lambda e: e.activation(out=HAL[:, :, 2 * bi + 1:2 * bi + 2], in_=Xv[:, :, 8 + t0 + T:8 + t0 + T + 1], func=AF.Identity),
                     reads=rd, writes=[b_hal])
            def cast_xbf(bi):
                t0, T = blocks[bi]
                xbf, bx = XBf[bi % 2], b_xbf[bi % 2]
                B.op("act", lambda e: e.activation(out=xbf[:, :, 1:T + 1], in_=Xv[:, :, 8 + t0:8 + t0 + T], func=AF.Identity),
                     reads=[b_x[bi]], writes=[bx])
                B.op("act", lambda e: e.activation(out=xbf[:, :, 0:1], in_=HAL[:, :, 2 * bi:2 * bi + 1], func=AF.Identity),
                     reads=[b_hal], writes=[bx])
                B.op("act", lambda e: e.activation(out=xbf[:, :, T + 1:T + 2], in_=HAL[:, :, 2 * bi + 1:2 * bi + 2], func=AF.Identity),
                     reads=[b_hal], writes=[bx])

            cast_xbf(0)
            deferred = []
            for bi, (t0, T) in enumerate(blocks):
                Z, b_z = zw[0][bi % 2], zw[6][bi % 2]
                xbf, bx = XBf[bi % 2], b_xbf[bi % 2]

                def bpair(j):
                    return ((0, 1), (2, 3), (6, 7))[j % 3]

                def emit_pe(j):
                    ui = bi * NPAIR + j
                    up_prefetch(ui + 3)
                    up, bup = UP[ui % 4], b_up[ui % 4]
                    bv, bg = bpair(j)
                    B.group("pe", [(lambda e, k=k: e.matmul(PS[:, bv, :T + 2], lhsT=up[:, k, 0:128], rhs=xbf[:, k, :T + 2],
                                                             start=(k == 0), stop=(k == 7))) for k in range(8)],
                            reads=[bup, bx], writes=[bank_b[bv]])
                    B.group("pe", [(lambda e, k=k: e.matmul(PS[:, bg, :T + 2], lhsT=up[:, k, 128:256], rhs=xbf[:, k, :T + 2],
                                                             start=(k == 0), stop=(k == 7))) for k in range(8)],
                            reads=[bup, bx], writes=[bank_b[bg]])

                def emit_center(j):
                    s2, s3 = j % 2, j % 3
                    bv, bg = bpair(j)
                    chv, chg = j, NPAIR + j
                    B.op("act", lambda e: e.activation(out=CV[s3][:, :T], in_=PS[:, bv, 1:T + 1], func=AF.Identity, bias=cb(chv), scale=cw(chv, 1)),
                         reads=[bank_b[bv], b_const], writes=[b_cv[s3]])
                    B.op("act", lambda e: e.activation(out=CGt[s2][:, :T], in_=PS[:, bg, 1:T + 1], func=AF.Identity, bias=cb(chg), scale=cw(chg, 1)),
                         reads=[bank_b[bg], b_const], writes=[b_cg2[s2]])

                def emit_taps_gelu(j):
                    s2, s3 = j % 2, j % 3
                    bv, bg = bpair(j)
                    chv, chg = j, NPAIR + j
                    for tap, lo in ((0, 0), (2, 2)):
                        B.op("dve", lambda e, tap=tap, lo=lo: e.scalar_tensor_tensor(out=CV[s3][:, :T], in0=PS[:, bv, lo:lo + T], scalar=cw(chv, tap),
                                                                                     in1=CV[s3][:, :T], op0=ALU.mult, op1=ALU.add),
                             reads=[bank_b[bv], b_cv[s3], b_const], writes=[b_cv[s3]])
                        B.op("dve", lambda e, tap=tap, lo=lo: e.scalar_tensor_tensor(out=CGt[s2][:, :T], in0=PS[:, bg, lo:lo + T], scalar=cw(chg, tap),
                                                                                     in1=CGt[s2][:, :T], op0=ALU.mult, op1=ALU.add),
                             reads=[bank_b[bg], b_cg2[s2], b_const], writes=[b_cg2[s2]])
                    B.op("act", lambda e: e.activation(out=GL[s2][:, :T], in_=CGt[s2][:, :T], func=AF.Gelu_apprx_tanh),
                         reads=[b_cg2[s2]], writes=[b_gl[s2]])

                def emit_mult(jm):
                    B.op("dve", lambda e: e.tensor_tensor(out=GTt[:, jm, :T], in0=GL[jm % 2][:, :T], in1=CV[jm % 3][:, :T], op=ALU.mult),
                         reads=[b_gl[jm % 2], b_cv[jm % 3]], writes=[b_gt])

                dn_prefetch(bi * 8 + 1)
                emit_pe(0)
                emit_pe(1)
                emit_center(0)
                for j in range(NPAIR):
                    if j + 2 < NPAIR:
                        emit_pe(j + 2)
                    if j + 1 < NPAIR:
                        emit_center(j + 1)
                    emit_taps_gelu(j)
                    if j >= 1:
                        emit_mult(j - 1)
                    if j in (2, 4, 6, 8, 10) and deferred:
                        deferred.pop(0)()
                emit_mult(NPAIR - 1)
                if bi + 1 < nblk:
                    cast_xbf(bi + 1)
                for oc in range(8):
                    di = bi * 8 + oc
                    dn_prefetch(di + 3)
                    dn, bdn = DN[di % 4], b_dn[di % 4]
                    bk = 4 + (oc % 2)
                    B.group("pe", [(lambda e, k=k: e.matmul(PS[:, bk, :T], lhsT=dn[:, k, :], rhs=GTt[:, k, :T],
                                                             start=(k == 0), stop=(k == NPAIR - 1))) for k in range(NPAIR)],
                            reads=[bdn, b_gt], writes=[bank_b[bk]])
                    B.op("dve", lambda e, oc=oc, bk=bk: e.scalar_tensor_tensor(out=Z[:, oc, :T], in0=Xv[:, oc, 8 + t0:8 + t0 + T], scalar=ALPHA,
                                                                               in1=PS[:, bk, :T], op0=ALU.mult, op1=ALU.add),
                         reads=[b_x[bi], bank_b[bk]], writes=[b_z])

                def tail(bi=bi, t0=t0, T=T):
                    if final:
                        out_toks.append(B.dma("sp", sg.yT[:, :, t0:t0 + T], Xv[:, :, 8 + t0:8 + t0 + T], reads=[b_x[bi]]))
                deferred.extend(ln_steps(zw, T, l, 2, Xv, b_x[bi], t0, bi % 2, banks=(4, 5)))
                deferred.append(tail)
            while deferred:
                deferred.pop(0)()
            B.barrier()

        aw = Alloc(OFF_W, WSZ)
        Wao = aw.get([128, 8, 1024], BF16)
        zw = zwork(aw)
        b_wao = Buf("wao")
        bwao_l = split_dma("pool", Wao.rearrange("p k n -> p (k n)"), w_ao, 8192, "wao")
        B.op("dve", lambda e: e.memset(Xv[:, :, 0:8], 0.0), writes=[b_xpad])
        B.op("dve", lambda e: e.memset(Xv[:, :, 8 + S:16 + S], 0.0), writes=[b_xpad])
        for bi, (t0, T) in enumerate(blocks):
            B.dma("sp", Xv[:, :, 8 + t0:8 + t0 + T], sg.xT[:, :, qoff + t0:qoff + t0 + T], writes=[b_x[bi]])
        def outproj_b(bi):
            t0, T = blocks[bi]
            Z, b_z = zw[0][bi % 2], zw[6][bi % 2]
            for oc in range(8):
                bk = rbank("b", [0, 1, 2, 3])
                B.group("pe", [(lambda e, k=k, oc=oc, bk=bk: e.matmul(PS[:, bk, :T], lhsT=Wao[:, k, oc * 128:(oc + 1) * 128],
                                                                      rhs=OT[:, k, t0:t0 + T], start=(k == 0), stop=(k == 7))) for k in range(8)],
                        reads=bwao_l + [b_ot], writes=[bank_b[bk]])
                B.op("dve", lambda e, oc=oc, bk=bk: e.scalar_tensor_tensor(out=Z[:, oc, :T], in0=Xv[:, oc, 8 + t0:8 + t0 + T], scalar=ALPHA,
                                                                           in1=PS[:, bk, :T], op0=ALU.mult, op1=ALU.add),
                     reads=[b_x[bi], bank_b[bk]], writes=[b_z])

        outproj_b(0)
        for bi, (t0, T) in enumerate(blocks):
            if bi + 1 < len(blocks):
                outproj_b(bi + 1)
            ln_block(zw, T, 0, 0, Xv, b_x[bi], t0, bi % 2)
        B.barrier()

        def dump_x():
            for c in range(8):
                B.dma("sp", dbg_out[:, c, :S], Xv[:, c, 8:8 + S], reads=b_x)

        if dbg == "ln1" and sg is segs[DBG_SEG]:
            dump_x(); break
        stage_xattn(0)
        if dbg == "ln2" and sg is segs[DBG_SEG]:
            dump_x(); break
        stage_ffn(0, False)
        if dbg == "l0" and sg is segs[DBG_SEG]:
            dump_x(); break

        ao = Alloc(OFF_O, OSZ)
        aw = Alloc(OFF_W, WSZ)
        PB = ao.get([128, 8, S], BF16)
        WT = [aw.get([128, 2, TBMAX + 16], F32) for _ in range(4)]
        b_pb = Buf("pb")
        b_wt = Buf("wt")
        WINS = (2, 4, 8, 16)
        nblk = len(blocks)
        for bi, (t0, T) in enumerate(blocks):
            rd = [b_x[bi], b_xpad] + ([b_x[bi - 1]] if bi > 0 else []) + ([b_x[bi + 1]] if bi + 1 < nblk else [])
            for gi in range(4):
                cs = slice(2 * gi, 2 * gi + 2)
                X0 = 8 + t0
                ext = {0: (0,), 1: (1, 0), 2: (3, 2, 0), 3: (7, 6, 4, 0)}[gi]
                e1 = ext[0]
                n1 = T + 2 * e1
                B.op("dve", lambda e: e.tensor_tensor(out=WT[0][:, :, :n1], in0=Xv[:, cs, X0 - e1 - 1:X0 - e1 - 1 + n1],
                                                       in1=Xv[:, cs, X0 - e1:X0 - e1 + n1], op=ALU.add), reads=rd + [b_wt], writes=[b_wt])
                cur, cure = WT[0], e1
                for lev in range(1, gi + 1):
                    sh = 2 ** (lev - 1)
                    en = ext[lev]
                    nn = T + 2 * en
                    o0 = cure - en
                    nxt = WT[lev]
                    B.op("dve", lambda e, cur=cur, nxt=nxt, o0=o0, sh=sh, nn=nn: e.tensor_tensor(
                        out=nxt[:, :, :nn], in0=cur[:, :, o0 - sh:o0 - sh + nn], in1=cur[:, :, o0 + sh:o0 + sh + nn], op=ALU.add),
                        reads=[b_wt], writes=[b_wt])
                    cur, cure = nxt, en
                B.op("dve", lambda e, cur=cur: e.scalar_tensor_tensor(out=PB[:, cs, t0:t0 + T], in0=cur[:, :, :T], scalar=1.0 / WINS[gi],
                                                                       in1=Xv[:, cs, X0:X0 + T], op0=ALU.mult, op1=ALU.subtract),
                     reads=rd + [b_wt], writes=[b_pb])
                if bi == 0:
                    B.op("dve", lambda e, cur=cur: e.tensor_tensor(out=cur[:, :, 0:8], in0=cur[:, :, 0:8], in1=RCE[:, cs, 0:8], op=ALU.mult),
                         reads=[b_wt, b_const], writes=[b_wt])
                    B.op("dve", lambda e, cur=cur: e.tensor_tensor(out=PB[:, cs, 0:8], in0=cur[:, :, 0:8], in1=Xv[:, cs, X0:X0 + 8], op=ALU.subtract),
                         reads=rd + [b_wt], writes=[b_pb])
                if bi == nblk - 1:
                    B.op("dve", lambda e, cur=cur: e.tensor_tensor(out=cur[:, :, T - 8:T], in0=cur[:, :, T - 8:T], in1=RCE[:, cs, 8:16], op=ALU.mult),
                         reads=[b_wt, b_const], writes=[b_wt])
                    B.op("dve", lambda e, cur=cur: e.tensor_tensor(out=PB[:, cs, S - 8:S], in0=cur[:, :, T - 8:T], in1=Xv[:, cs, X0 + T - 8:X0 + T],
                                                                    op=ALU.subtract), reads=rd + [b_wt], writes=[b_pb])
        B.barrier()
        aw = Alloc(OFF_W, WSZ)
        zw = zwork(aw)
        PW = aw.get([128, 4, 2, 256], BF16)
        MT4 = [aw.get([128, TBMAX], F32) for _ in range(4)]
        b_pw, b_mt4 = Buf("pw"), [Buf(f"mt{i}") for i in range(4)]
        B.dma("pool", PW.rearrange("p g k n -> p (g k n)"), w_pool[:, :], writes=[b_pw])
        def pool_mm(bi):
            t0, T = blocks[bi]
            Z, b_z = zw[0][bi % 2], zw[6][bi % 2]
            for oc in range(8):
                gi, ol = oc // 2, oc % 2
                bk = rbank("b", [0, 1, 2, 3])
                MT_, b_mt_ = MT4[oc % 4], b_mt4[oc % 4]
                B.group("pe", [(lambda e, kc=kc: e.matmul(PS[:, bk, :T], lhsT=PW[:, gi, kc, ol * 128:(ol + 1) * 128],
                                                           rhs=PB[:, 2 * gi + kc, t0:t0 + T], start=(kc == 0), stop=(kc == 1))) for kc in range(2)],
                        reads=[b_pw, b_pb], writes=[bank_b[bk]])
                B.op("act", lambda e: e.activation(out=MT_[:, :T], in_=PS[:, bk, :T], func=AF.Identity, bias=DER[:, oc:oc + 1],
                                                   scale=PV[:, PV_PS + oc:PV_PS + oc + 1]), reads=[bank_b[bk], b_const], writes=[b_mt_])
                B.op("dve", lambda e: e.scalar_tensor_tensor(out=Z[:, oc, :T], in0=Xv[:, oc, 8 + t0:8 + t0 + T], scalar=ALPHA,
                                                              in1=MT_[:, :T], op0=ALU.mult, op1=ALU.add),
                     reads=[b_x[bi], b_mt_], writes=[b_z])

        pool_mm(0)
        for bi, (t0, T) in enumerate(blocks):
            if bi + 1 < len(blocks):
                pool_mm(bi + 1)
            ln_block(zw, T, 1, 0, Xv, b_x[bi], t0, bi % 2)
        B.barrier()
        if dbg == "l1ln1" and sg is segs[DBG_SEG]:
            dump_x(); break
        stage_xattn(1)
        stage_ffn(1, True)

    B.barrier()
    return B


DBG_SEG = 0


def _kp(w):
    K = w.shape[0] // 128
    return np.ascontiguousarray(w.reshape(K, 128, w.shape[1]).transpose(1, 0, 2).reshape(128, -1))


def _rope_tab(pos_tokens):
    t = np.asarray(pos_tokens)
    row = (t // GRID_W).astype(np.float32)
    col = (t % GRID_W).astype(np.float32)
    inv = (np.float32(10000.0) ** (-np.arange(16, dtype=np.float32) / np.float32(16))).astype(np.float32)
    tab = np.zeros((64, 2, t.shape[0]), np.float32)
    for i in range(64):
        pos = row if i < 32 else col
        ii = i % 32
        ang = (pos * inv[ii % 16]).astype(np.float32)
        tab[i, 0] = np.cos(ang)
        tab[i, 1] = -np.sin(ang) if ii < 16 else np.sin(ang)
    return np.ascontiguousarray(np.concatenate([tab, tab], axis=0))


def _partner():
    i = np.arange(64)
    ii = i % 32
    return np.where(ii < 16, i + 16, i - 16)


def _na_masks(kind, half):
    c = np.arange(64)
    cs = np.clip(c - 8, 0, 48)
    colv = (c[None, :] >= cs[:, None]) & (c[None, :] < cs[:, None] + 16)
    m = np.zeros((128, 16, 64), np.float32)
    for p in range(128):
        kc = p % 64
        for j in range(16):
            dr = j - 8 + (1 if p >= 64 else 0)
            ok = -7 <= dr <= 7
            if kind == "sample":
                ok = ok and (dr >= -4 if half == 0 else dr <= 3)
            if ok:
                m[p, j, :] = colv[:, kc].astype(np.float32)
    neg = np.where(m > 0, np.float32(0.0), np.float32(NEG)).astype(np.float32)
    return m.reshape(128, 1024), neg.reshape(128, 1024)


def _na_gather(rpb):
    p = np.arange(128)
    kc = p % 64
    j = np.arange(16)
    qc = np.arange(64)
    dr = j[None, :] - 8 + (p[:, None] >= 64)
    dri = np.clip(dr + 7, 0, 14)
    dci = np.clip(kc[:, None] - qc[None, :] + 15, 0, 30)
    g = rpb[:, dri[:, :, None], dci[:, None, :]]
    return np.ascontiguousarray(g.transpose(1, 0, 2, 3).reshape(128, -1)).astype(np.float32)


_CACHE = {}


def kernel(x_prompt, x_sample, mem_prompt, mem_sample, ab_w_in, na_rpb, gqa_q_gain, gqa_k_gain,
           ab_w_out, pool_w, pool_b, pool_scale, ln1_g, ln1_b, xa_wq, xa_wkv, xa_wo, ln2_g, ln2_b,
           ffn_w_up, ffn_conv_w, ffn_conv_b, ffn_w_down, ln3_g, ln3_b, _dbg=None):
    f = lambda a: np.asarray(a, dtype=np.float32)
    x_prompt, x_sample, mem_prompt, mem_sample = f(x_prompt), f(x_sample), f(mem_prompt), f(mem_sample)
    w_in = f(ab_w_in)[0]
    pt = _partner()
    shared = {}
    shared["w_na"] = _kp(w_in[:, 0:1536])
    qcols = w_in[:, 1536:2048].reshape(D, 8, 64)
    kcols = w_in[:, 2048:2176].reshape(D, 2, 64)
    vcols = w_in[:, 2176:2304]
    kdup = np.concatenate([kcols, kcols], axis=2)
    krot = kcols[:, :, pt]
    krdup = np.concatenate([krot, krot], axis=2)
    wg = np.concatenate([qcols.reshape(D, 512), qcols[:, :, pt].reshape(D, 512), kdup.reshape(D, 256),
                         krdup.reshape(D, 256), vcols], axis=1)
    shared["w_gq"] = _kp(wg)
    shared["w_ao"] = _kp(f(ab_w_out)[0])
    for l in range(2):
        shared[f"w_xq{l}"] = _kp(f(xa_wq)[l])
        shared[f"w_xo{l}"] = _kp(f(xa_wo)[l])
        wkv = f(xa_wkv)[l]
        shared[f"w_xkv{l}"] = np.ascontiguousarray(np.stack([_kp(wkv[:, s * 256:(s + 1) * 256]) for s in range(8)]))
        wu = f(ffn_w_up)[l]
        shared[f"w_up{l}"] = np.ascontiguousarray(np.stack(
            [_kp(np.concatenate([wu[:, j * 128:(j + 1) * 128], wu[:, DFF + j * 128:DFF + (j + 1) * 128]], axis=1)) for j in range(NPAIR)]))
        wd = f(ffn_w_down)[l]
        shared[f"w_dn{l}"] = np.ascontiguousarray(np.stack([_kp(wd[:, oc * 128:(oc + 1) * 128]) for oc in range(8)]))
    pw = f(pool_w)[0]
    shared["w_pool"] = np.ascontiguousarray(pw.reshape(4, 2, 128, 256).transpose(2, 0, 1, 3).reshape(128, -1))
    cst = np.zeros((128, 384), np.float32)
    cst[:, 0:128] = np.eye(128, dtype=np.float32)
    cst[0:64, 128:192] = 1.0
    cst[64:128, 192:256] = 1.0
    shared["consts"] = cst
    pv = np.zeros((128, 512), np.float32)
    lns = [(f(ln1_g), f(ln1_b)), (f(ln2_g), f(ln2_b)), (f(ln3_g), f(ln3_b))]
    for l in range(2):
        for n in range(3):
            o = ((l * 3 + n) * 2) * 8
            pv[:, o:o + 8] = lns[n][0][l].reshape(8, 128).T
            pv[:, o + 8:o + 16] = lns[n][1][l].reshape(8, 128).T
        cwl = f(ffn_conv_w)[l]
        pv[:, 96 + l * 132:96 + (l + 1) * 132] = cwl.reshape(3, 44, 128).transpose(2, 1, 0).reshape(128, 132)
        pv[:, 360 + l * 44:360 + (l + 1) * 44] = f(ffn_conv_b)[l].reshape(44, 128).T
    pv[:, 448:456] = f(pool_b)[0].reshape(8, 128).T
    pv[:, 456:464] = f(pool_scale)[0].reshape(8, 128).T
    gq, gk = f(gqa_q_gain)[0], f(gqa_k_gain)[0]
    pv[:, 464] = np.concatenate([gq, gq]); pv[:, 465] = np.concatenate([gq[pt], gq[pt]])
    pv[:, 466] = np.concatenate([gk, gk]); pv[:, 467] = np.concatenate([gk[pt], gk[pt]])
    shared["pvec"] = pv
    shared["na_g"] = _na_gather(f(na_rpb)[0])
    shared["na_m_p"], shared["na_n_p"] = _na_masks("prompt", 0)
    shared["rope_p"] = _rope_tab(np.arange(2048))
    rce = np.ones((128, 8, 16), np.float32)
    for gi, win in enumerate((2, 4, 8, 16)):
        bl, br = win // 2, win - 1 - win // 2
        for e in range(8):
            rce[:, 2 * gi:2 * gi + 2, e] = 1.0 / (min(e, bl) + 1 + br)
            rce[:, 2 * gi:2 * gi + 2, 8 + e] = 1.0 / (bl + 1 + min(7 - e, br))
    shared["rc_edge"] = rce.reshape(128, 128)

    in_maps = []
    for c in range(NCORES):
        m = dict(shared)
        for i in range(2):
            b = 2 * c + i
            m[f"xT_p{i}"] = np.ascontiguousarray(x_prompt[b].T)
            m[f"memT_p{i}"] = _kp(np.ascontiguousarray(mem_prompt[b].T))
        sb, half = c // 2, c % 2
        w0 = 0 if half == 0 else 1920
        e0 = w0 - 256
        tok_e = np.arange(e0, e0 + 2688)
        ex_e = (tok_e >= 0) & (tok_e < 4096)
        tok_r = np.arange(2432, 4096) if half == 0 else np.arange(0, 1664)
        toks = np.concatenate([tok_e, tok_r])
        ex = np.concatenate([ex_e, np.ones(tok_r.shape[0], bool)])
        xs = np.zeros((toks.shape[0], D), np.float32)
        xs[ex] = x_sample[sb][toks[ex]]
        m["xT_s"] = np.ascontiguousarray(xs.T)
        m["memT_s"] = _kp(np.ascontiguousarray(mem_sample[sb].T))
        m["ropeq_s"] = _rope_tab(np.arange(w0, w0 + 2176))
        m["ropek_s"] = _rope_tab(np.clip(toks, 0, 4095))
        m["na_m_s"], m["na_n_s"] = _na_masks("sample", half)
        m["exE_s"] = np.ascontiguousarray(ex_e.astype(np.float32).reshape(21, 128).T)
        m["exK_s"] = np.ascontiguousarray(ex.astype(np.float32).reshape(34, 128).T)
        in_maps.append(m)

    if _dbg is not None and _dbg.startswith(("na", "consts")):
        for m in in_maps:
            for k in list(m.keys()):
                if k.startswith(("w_gq", "w_ao", "w_x", "w_up", "w_dn", "w_pool")):
                    m[k] = np.zeros((2, 2), np.float32)
    key = _dbg
    if key not in _CACHE:
        _CACHE[key] = build_program(_dbg)
    B = _CACHE[key]
    res = run_bass_kernel_spmd(B.nc, in_maps, core_ids=list(range(NCORES)))
    if _dbg:
        return [r["dbg"] for r in res.results]
    y_prompt = np.empty((16, 2048, D), np.float32)
    y_sample = np.empty((4, 4096, D), np.float32)
    for c in range(NCORES):
        r = res.results[c]
        for i in range(2):
            y_prompt[2 * c + i] = r[f"yT_p{i}"].T
        sb, half = c // 2, c % 2
        ys = r["yT_s"]
        if half == 0:
            y_sample[sb, 0:2048] = ys[:, 0:2048].T
        else:
            y_sample[sb, 2048:4096] = ys[:, 128:2176].T
    return (y_prompt, y_sample)
```
